# Optimizing a Trainium2 kernel written in Bass

```python
import jax, jax.numpy as jnp
from jax import lax
import numpy as np

D_MODEL = 2048
BATCH = 2
SEQ = 4096
DEPTH = 1

GRID_W = 64
CTX_LEN = 256
M_HEADS = 4
M_V = D_MODEL // 2
M_DV = M_V // M_HEADS
M_QK = M_V // 2
M_DQK = M_QK // M_HEADS
M_CHUNK = 64
R_W = D_MODEL // 2
R_N = 64
R_HEADS = R_W // R_N
R_LORA_W = 64
R_LORA_A = 64
R_LORA_G = 160
D_MIX = M_V + R_W
SPLIT_SIZES = (M_QK, M_QK, M_V, M_V, 4 * M_HEADS, R_W, R_W, R_W, 2 * R_LORA_W, 2 * R_LORA_A, R_LORA_G)
N_IN = 2 * M_QK + 2 * M_V + 4 * M_HEADS + 3 * R_W + 2 * R_LORA_W + 2 * R_LORA_A + R_LORA_G
D_FF = ((8 * D_MODEL // 3 + 255) // 256) * 256
NORM_EPS = 1e-6
GN_EPS = 64e-5

kernel_name = "hybrid_mlstm_rwkv7_prefix_dit_block"


def rms_norm(x, g):
    x32 = x.astype(jnp.float32)
    y = x32 * lax.rsqrt(jnp.mean(x32 * x32, axis=-1, keepdims=True) + NORM_EPS)
    return (y * g.astype(jnp.float32)).astype(x.dtype)


def adaln(cvec, w, b):
    mod = (jax.nn.silu(cvec) @ w + b).reshape(-1, 1, 6 * D_MODEL)
    return jnp.split(mod, 6, axis=-1)


def modulate(h, shift, scale):
    return h * (1 + scale) + shift


def conv1d_centred(x, w, b):
    xp = jnp.pad(x, ((0, 0), (1, 1), (0, 0)))
    return xp[:, :-2] * w[0] + xp[:, 1:-1] * w[1] + xp[:, 2:] * w[2] + b


def mlstm_chunkwise(q, k, v, i_pre, logf, state):
    B, H, L, _ = q.shape
    nc = L // M_CHUNK
    to_chunks = lambda t: jnp.moveaxis(t.reshape((B, H, nc, M_CHUNK) + t.shape[3:]), 2, 0)
    tri = jnp.tril(jnp.ones((M_CHUNK, M_CHUNK), dtype=bool))

    def body(carry, xs):
        C, n, m = carry
        qc, kc, vc, ic, fc = xs
        bcum = jnp.cumsum(fc, axis=-1)
        dmat = bcum[..., :, None] - bcum[..., None, :] + ic[..., None, :]
        dmat = jnp.where(tri, dmat, -jnp.inf)
        m_inter = bcum + m[..., None]
        m_j = jnp.maximum(m_inter, jnp.max(dmat, axis=-1))
        s = jnp.einsum("bhjd,bhsd->bhjs", qc, kc) * jnp.exp(dmat - m_j[..., None])
        inter = jnp.exp(m_inter - m_j)
        num = jnp.einsum("bhjs,bhsv->bhjv", s, vc) + inter[..., None] * jnp.einsum("bhvd,bhjd->bhjv", C, qc)
        den = jnp.sum(s, axis=-1) + inter * jnp.einsum("bhd,bhjd->bhj", n, qc)
        h = num / jnp.maximum(jnp.abs(den), jnp.exp(-m_j))[..., None]
        b_last = bcum[..., -1]
        glog = b_last[..., None] - bcum + ic
        m_new = jnp.maximum(b_last + m, jnp.max(glog, axis=-1))
        wk = jnp.exp(glog - m_new[..., None])
        decay = jnp.exp(b_last + m - m_new)
        C_new = decay[..., None, None] * C + jnp.einsum("bhs,bhsv,bhsd->bhvd", wk, vc, kc)
        n_new = decay[..., None] * n + jnp.einsum("bhs,bhsd->bhd", wk, kc)
        return (C_new, n_new, m_new), h

    xs = (to_chunks(q), to_chunks(k), to_chunks(v), to_chunks(i_pre), to_chunks(logf))
    final, hs = lax.scan(body, state, xs)
    h = jnp.moveaxis(hs, 0, 2).reshape(B, H, L, v.shape[-1])
    return h, final


def rwkv7_scan(r, w, k, v, kk, a, s0, reverse):
    def step(s, xs):
        r_t, w_t, k_t, v_t, kk_t, a_t = xs
        s = (s * w_t[:, :, None, :]
             - jnp.einsum("bhvk,bhk->bhv", s, kk_t)[..., None] * (kk_t * a_t)[:, :, None, :]
             + v_t[..., :, None] * k_t[..., None, :])
        return s, jnp.einsum("bhvk,bhk->bhv", s, r_t)
    xs = tuple(jnp.moveaxis(t, 1, 0) for t in (r, w, k, v, kk, a))
    s_final, ys = lax.scan(step, s0, xs, reverse=reverse)
    return jnp.moveaxis(ys, 0, 1), s_final


def zero_state(batch):
    mz = (jnp.zeros((batch, M_HEADS, M_DV, M_DQK), jnp.float32),
          jnp.zeros((batch, M_HEADS, M_DQK), jnp.float32),
          jnp.zeros((batch, M_HEADS), jnp.float32))
    sz = jnp.zeros((batch, R_HEADS, R_N, R_N), jnp.float32)
    return (mz, mz, sz, sz)


def hybrid_mixers(h, p, state):
    B, L, _ = h.shape
    proj = (h @ p["w_in"]).astype(jnp.float32)
    (q, k, v_m, o_m, gates, r, k_r, v_r, lw, la, lg) = jnp.split(
        proj, np.cumsum(SPLIT_SIZES)[:-1].tolist(), axis=-1)

    qk = conv1d_centred(jnp.concatenate([q, k], axis=-1), p["m_conv_w"], p["m_conv_b"])
    q, k = jnp.split(qk, 2, axis=-1)
    heads = lambda t, n: t.reshape(B, L, -1, n).transpose(0, 2, 1, 3)
    q = heads(q, M_DQK) * (M_DQK ** -0.5)
    k = heads(k, M_DQK)
    vm = heads(v_m, M_DV)
    g4 = (gates.reshape(B, L, 4, M_HEADS) + p["m_gate_b"]).transpose(2, 0, 3, 1)
    i_f, i_b = g4[0], g4[1]
    lf_f, lf_b = jax.nn.log_sigmoid(g4[2]), jax.nn.log_sigmoid(g4[3])
    flip = lambda t: jnp.flip(t, axis=2)
    h_f, st_mf = mlstm_chunkwise(q, k, vm, i_f, lf_f, state[0])
    h_b, st_mb = mlstm_chunkwise(flip(q), flip(k), flip(vm), flip(i_b), flip(lf_b), state[1])
    hm = h_f + flip(h_b)
    hm = hm * lax.rsqrt(jnp.mean(hm * hm, axis=-1, keepdims=True) + NORM_EPS)
    hm = hm.transpose(0, 2, 1, 3).reshape(B, L, M_V) * p["m_norm_g"] * jax.nn.sigmoid(o_m)

    lw = lw.reshape(B, L, 2, R_LORA_W)
    la = la.reshape(B, L, 2, R_LORA_A)
    wlog = -jax.nn.softplus(-(p["r_w0"] + jnp.einsum("bldr,drc->bldc", jnp.tanh(lw), p["r_w_up"]))) - 0.5
    decay = jnp.exp(-jnp.exp(wlog))
    a = jax.nn.sigmoid(p["r_a0"] + jnp.einsum("bldr,drc->bldc", la, p["r_a_up"]))
    g = jax.nn.sigmoid(lg) @ p["r_g_up"]
    hd = lambda t: t.reshape(B, L, R_HEADS, R_N)
    kk = hd(k_r * p["r_k_k"])
    kk = kk * lax.rsqrt(jnp.sum(kk * kk, axis=-1, keepdims=True) + 1e-12)
    k_dir = k_r[:, :, None, :] * (1 + (a - 1) * p["r_k_a"])
    rh, vh = hd(r), hd(v_r)
    y_f, s_f = rwkv7_scan(rh, hd(decay[:, :, 0]), hd(k_dir[:, :, 0]), vh, kk, hd(a[:, :, 0]), state[2], False)
    y_b, s_b = rwkv7_scan(rh, hd(decay[:, :, 1]), hd(k_dir[:, :, 1]), vh, kk, hd(a[:, :, 1]), state[3], True)
    y = y_f + y_b
    mu = jnp.mean(y, axis=-1, keepdims=True)
    var = jnp.mean(jnp.square(y - mu), axis=-1, keepdims=True)
    y = ((y - mu) * lax.rsqrt(var + GN_EPS)).reshape(B, L, R_W) * p["r_gn_w"] + p["r_gn_b"]
    bonus = jnp.sum(rh * hd(k_dir[:, :, 0] + k_dir[:, :, 1]) * p["r_r_k"], axis=-1, keepdims=True) * vh
    y = (y + bonus.reshape(B, L, R_W)) * g

    out = jnp.concatenate([hm, y], axis=-1).astype(h.dtype)
    return out, (st_mf, st_mb, s_f, s_b)


def conv_ffn(h, rows, cols, p):
    B, L, _ = h.shape
    u = (h @ p["f_w_up"]).reshape(B, rows, cols, D_FF)
    gt = h @ p["f_w_gate"]
    kern = p["f_conv_w"][:, :, None, :].astype(u.dtype)
    u = lax.conv_general_dilated(u, kern, (1, 1), "SAME",
                                 dimension_numbers=("NHWC", "HWIO", "NHWC"),
                                 feature_group_count=D_FF) + p["f_conv_b"]
    u = u.reshape(B, L, D_FF)
    return (jax.nn.gelu(u, approximate=True) * gt) @ p["f_w_down"]


def setup_inputs(seed: int = 0) -> dict:
    key = jax.random.key(seed)
    ks = jax.random.split(key, 32)
    nrm = lambda k, shape, s: jax.random.normal(k, shape, jnp.float32) * s
    D = D_MODEL
    m_gate_b = jnp.concatenate([
        nrm(ks[9], (DEPTH, 2, M_HEADS), 0.1),
        3.0 + 3.0 * jax.random.uniform(ks[10], (DEPTH, 2, M_HEADS), jnp.float32)], axis=1)
    return {
        "x": nrm(ks[0], (BATCH, SEQ, D), 1.0),
        "c": nrm(ks[1], (BATCH, D), 1.0),
        "ctx": nrm(ks[2], (BATCH, CTX_LEN, D), 1.0),
        "c_ctx": nrm(ks[3], (D,), 1.0),
        "w_mod": nrm(ks[4], (DEPTH, D, 6 * D), 0.5 * D ** -0.5),
        "b_mod": nrm(ks[5], (DEPTH, 6 * D), 0.02),
        "g_norm1": 1.0 + nrm(ks[6], (DEPTH, D), 0.02),
        "g_norm2": 1.0 + nrm(ks[7], (DEPTH, D), 0.02),
        "w_in": nrm(ks[8], (DEPTH, D, N_IN), D ** -0.5),
        "m_conv_w": nrm(ks[11], (DEPTH, 3, 2 * M_QK), 3 ** -0.5),
        "m_conv_b": nrm(ks[12], (DEPTH, 2 * M_QK), 0.02),
        "m_gate_b": m_gate_b,
        "m_norm_g": 1.0 + nrm(ks[13], (DEPTH, M_V), 0.02),
        "r_w0": jax.random.uniform(ks[14], (DEPTH, 2, R_W), jnp.float32, -6.0, 0.0),
        "r_w_up": nrm(ks[15], (DEPTH, 2, R_LORA_W, R_W), 0.5 * R_LORA_W ** -0.5),
        "r_a0": nrm(ks[16], (DEPTH, 2, R_W), 0.1),
        "r_a_up": nrm(ks[17], (DEPTH, 2, R_LORA_A, R_W), 0.5 * R_LORA_A ** -0.5),
        "r_g_up": nrm(ks[18], (DEPTH, R_LORA_G, R_W), R_LORA_G ** -0.5),
        "r_k_k": 0.85 + nrm(ks[19], (DEPTH, R_W), 0.05),
        "r_k_a": 1.0 + nrm(ks[20], (DEPTH, R_W), 0.05),
        "r_r_k": nrm(ks[21], (DEPTH, R_HEADS, R_N), 0.1),
        "r_gn_w": 1.0 + nrm(ks[22], (DEPTH, R_W), 0.02),
        "r_gn_b": nrm(ks[23], (DEPTH, R_W), 0.02),
        "w_out": nrm(ks[24], (DEPTH, D_MIX, D), D_MIX ** -0.5),
        "f_w_up": nrm(ks[25], (DEPTH, D, D_FF), D ** -0.5),
        "f_w_gate": nrm(ks[26], (DEPTH, D, D_FF), D ** -0.5),
        "f_conv_w": nrm(ks[27], (DEPTH, 3, 3, D_FF), 1.0 / 3.0),
        "f_conv_b": nrm(ks[28], (DEPTH, D_FF), 0.02),
        "f_w_down": nrm(ks[29], (DEPTH, D_FF, D), D_FF ** -0.5),
        "g_final": 1.0 + nrm(ks[30], (D,), 0.02),
    }


def reference(x, c, ctx, c_ctx, w_mod, b_mod, g_norm1, g_norm2, w_in, m_conv_w, m_conv_b, m_gate_b,
              m_norm_g, r_w0, r_w_up, r_a0, r_a_up, r_g_up, r_k_k, r_k_a, r_r_k, r_gn_w, r_gn_b,
              w_out, f_w_up, f_w_gate, f_conv_w, f_conv_b, f_w_down, g_final):
    B, L, _ = x.shape
    rows = L // GRID_W
    ctx_len = ctx.shape[1]
    for layer in range(DEPTH):
        last = layer == DEPTH - 1
        p = {
            "w_in": w_in[layer], "m_conv_w": m_conv_w[layer], "m_conv_b": m_conv_b[layer],
            "m_gate_b": m_gate_b[layer], "m_norm_g": m_norm_g[layer],
            "r_w0": r_w0[layer], "r_w_up": r_w_up[layer], "r_a0": r_a0[layer], "r_a_up": r_a_up[layer],
            "r_g_up": r_g_up[layer], "r_k_k": r_k_k[layer], "r_k_a": r_k_a[layer], "r_r_k": r_r_k[layer],
            "r_gn_w": r_gn_w[layer], "r_gn_b": r_gn_b[layer],
            "f_w_up": f_w_up[layer], "f_w_gate": f_w_gate[layer], "f_conv_w": f_conv_w[layer],
            "f_conv_b": f_conv_b[layer], "f_w_down": f_w_down[layer],
        }
        sh1, sc1, gt1, sh2, sc2, gt2 = adaln(c, w_mod[layer], b_mod[layer])
        csh1, csc1, cgt1, csh2, csc2, cgt2 = adaln(c_ctx, w_mod[layer], b_mod[layer])

        hc = modulate(rms_norm(ctx, g_norm1[layer]), csh1, csc1)
        yc, ctx_state = hybrid_mixers(hc, p, zero_state(ctx.shape[0]))
        hx = modulate(rms_norm(x, g_norm1[layer]), sh1, sc1)
        yx, _ = hybrid_mixers(hx, p, ctx_state)
        x = x + gt1 * (yx @ w_out[layer])

        hx = modulate(rms_norm(x, g_norm2[layer]), sh2, sc2)
        x = x + gt2 * conv_ffn(hx, rows, GRID_W, p)

        if not last:
            ctx = ctx + cgt1 * (yc @ w_out[layer])
            hc = modulate(rms_norm(ctx, g_norm2[layer]), csh2, csc2)
            ctx = ctx + cgt2 * conv_ffn(hc, 1, ctx_len, p)
    return rms_norm(x, g_final)
```

```python
import numpy as np
from contextlib import ExitStack
import concourse.bass as bass
import concourse.mybir as mybir
from concourse.bass_utils import run_bass_kernel_spmd

F32 = mybir.dt.float32
BF16 = mybir.dt.bfloat16
AF = mybir.ActivationFunctionType
ALU = mybir.AluOpType
AX = mybir.AxisListType

EPOCH = 8000
NDMA_SEM = 8
SAME_ENGINE_SYNC = True


class T:
    def __init__(self, ap, name=""):
        self.ap = ap
        self.name = name
        self.w = None
        self.r = []

    def __getitem__(self, idx):
        return self.ap[idx]


class _Rec:
    def __init__(self):
        self.calls = []

    def __getattr__(self, name):
        def f(*a, **k):
            self.calls.append((name, a, k))
            return self
        return f


class Prog:
    ENG = ("pe", "dve", "act", "pool", "sp")

    def __init__(self, nc):
        self.nc = nc
        self.es = ExitStack()
        self.count = {e: 0 for e in self.ENG}
        self.stream = {e: [] for e in self.ENG}
        self.sems = {}
        self.dma_sems = {}
        self.dma_count = {e: 0 for e in self.ENG}
        self.seen = {e: {f: 0 for f in self.ENG} for e in self.ENG}
        self.seen_dma = {e: set() for e in self.ENG}
        self.nwait = 0
        self.uid = 0

    def sem(self, name):
        return self.es.enter_context(self.nc.semaphore(name))

    def sbuf(self, name, shape, dtype=F32, es=None):
        t = (es or self.es).enter_context(self.nc.sbuf_tensor(name, list(shape), dtype))
        return T(t, name)

    def psum(self, name, shape, dtype=F32, es=None):
        t = (es or self.es).enter_context(self.nc.psum_tensor(name, list(shape), dtype))
        return T(t, name)

    def dram(self, name, shape, dtype=F32, kind="Internal"):
        t = self.nc.dram_tensor(name, list(shape), dtype, kind=kind)
        return T(t.ap(), name)

    def view(self, ap, name=""):
        return T(ap, name)

    def _sem_of(self, eng, idx):
        ep = (idx - 1) // EPOCH
        key = (eng, ep)
        if key not in self.sems:
            self.sems[key] = self.sem(f"s_{eng}_{ep}")
        return self.sems[key], (idx - 1) % EPOCH + 1

    def _dma_sem_of(self, q, i):
        key = (q, i % NDMA_SEM)
        if key not in self.dma_sems:
            self.dma_sems[key] = self.sem(f"d_{q}_{i % NDMA_SEM}")
        return self.dma_sems[key], 16 * (i // NDMA_SEM + 1)

    def _wait(self, eng, dep):
        kind, e, i = dep
        if kind == "x":
            self.stream[eng].append(("w", e, i))
            return
        if kind == "c":
            if e == eng and (eng in ("pe", "sp") or not SAME_ENGINE_SYNC):
                return
            if self.seen[eng][e] >= i:
                return
            self.seen[eng][e] = i
            s, v = self._sem_of(e, i)
        else:
            if (e, i) in self.seen_dma[eng]:
                return
            self.seen_dma[eng].add((e, i))
            s, v = self._dma_sem_of(e, i)
        self.stream[eng].append(("w", s, v))
        self.nwait += 1

    def _deps(self, eng, reads, writes):
        deps = []
        for t in reads:
            if t.w is not None:
                deps.append(t.w)
        for t in writes:
            if t.w is not None:
                deps.append(t.w)
            deps.extend(t.r)
        for d in deps:
            self._wait(eng, d)

    def op(self, eng, fn, reads=(), writes=()):
        self._deps(eng, reads, writes)
        self.count[eng] += 1
        idx = self.count[eng]
        s, v = self._sem_of(eng, idx)
        rec = _Rec()
        fn(rec)
        assert len(rec.calls) == 1, rec.calls
        name, a_, k_ = rec.calls[0]
        self.stream[eng].append(("i", (lambda e, name=name, a_=a_, k_=k_: getattr(e, name)(*a_, **k_)), s, 1))
        me = ("c", eng, idx)
        for t in reads:
            t.r.append(me)
        for t in writes:
            t.w = me
            t.r = []
        return me

    def dma(self, out_ap, in_ap, reads=(), writes=(), q="sp", **kw):
        self._deps(q, reads, writes)
        i = self.dma_count[q]
        self.dma_count[q] += 1
        if i >= NDMA_SEM:
            self._wait(q, ("d", q, i - NDMA_SEM))
        s, v = self._dma_sem_of(q, i)
        rs = lambda a: a() if callable(a) else a
        def _do(e):
            o_, i_ = rs(out_ap), rs(in_ap)
            try:
                return e.dma_start(out=o_, in_=i_, **kw)
            except Exception:
                print("DMA FAIL", q, getattr(o_, "shape", None), getattr(i_, "shape", None), o_, i_)
                raise
        self.stream[q].append(("i", _do, s, 16))
        me = ("d", q, i)
        for t in reads:
            t.r.append(me)
        for t in writes:
            t.w = me
            t.r = []
        return me

    def raw(self, eng, fn):
        self.stream[eng].append(("r", fn))

    def coll(self, fn, reads=(), writes=()):
        self._deps("pool", reads, writes)
        sem = self.sem("csem%d" % self.uid)
        self.uid += 1
        self.stream["pool"].append(("c", fn, sem))
        me = ("x", sem, 1)
        for t in reads:
            t.r.append(me)
        for t in writes:
            t.w = me
            t.r = []
        return me

    def barrier(self):
        for eng in self.ENG:
            for f in self.ENG:
                if f != eng and self.count[f] > 0:
                    self._wait(eng, ("c", f, self.count[f]))
            for q in self.ENG:
                n = self.dma_count[q]
                for i in range(max(0, n - NDMA_SEM), n):
                    self._wait(eng, ("d", q, i))

    def final_wait(self, eng="sp"):
        for q in self.ENG:
            n = self.dma_count[q]
            for i in range(max(0, n - NDMA_SEM), n):
                self._wait(eng, ("d", q, i))

    def finish(self):
        nc = self.nc
        names = {"pe": "tensor", "dve": "vector", "act": "scalar", "pool": "gpsimd", "sp": "sync"}
        with nc.Block() as block:
            for eng in self.ENG:
                items = self.stream[eng]
                if not items:
                    continue

                def body(e, items=items):
                    for it in items:
                        if it[0] == "w":
                            e.wait_ge(it[1], it[2])
                        elif it[0] == "r":
                            it[1](e)
                        elif it[0] == "c":
                            it[1](e).then_inc(it[2])
                        else:
                            it[1](e).then_inc(it[2], it[3])
                getattr(block, names[eng])(body)
        self.es.close()

    def mm(self, out_t, out_ap, lhsT_t, lhsT_ap, rhs_t, rhs_ap, start=True, stop=True, **kw):
        return self.op("pe", lambda e: e.matmul(out_ap, lhsT_ap, rhs_ap, start=start, stop=stop, **kw),
                       reads=[lhsT_t, rhs_t], writes=[out_t])

    def tr(self, out_t, out_ap, in_t, in_ap, id_t, id_ap):
        return self.op("pe", lambda e: e.transpose(out_ap, in_ap, id_ap), reads=[in_t, id_t], writes=[out_t])


D = 2048
LX = 4096
CTX = 256
LT = LX + CTX
NTT = LT // 128
DFF = 5632
NF = DFF // 128
NFM = 932
NTM = 1280
NCOL = NFM + NTM
FM_CH = [(0, 128), (128, 128), (256, 128), (384, 128), (512, 128), (640, 32), (672, 128), (800, 128), (928, 4)]
EPS = 1e-6


def host_prep(inp, c):
    b, j = c // 4, c % 4
    f = lambda a: np.ascontiguousarray(a, dtype=np.float32)
    w_in = inp["w_in"][0]
    g = 3072
    cols_fm = np.concatenate([
        np.arange(j * 128, j * 128 + 128), 512 + np.arange(j * 128, j * 128 + 128),
        6160 + np.arange(128), 6288 + np.arange(128), 6416 + np.arange(160),
        5136 + np.arange(j * 256, j * 256 + 256),
        np.array([g + 0 * 4 + j, g + 1 * 4 + j, g + 2 * 4 + j, g + 3 * 4 + j])])
    cols_tm = np.concatenate([
        1024 + np.arange(j * 256, j * 256 + 256), 2048 + np.arange(j * 256, j * 256 + 256),
        3088 + np.arange(j * 256, j * 256 + 256), 4112 + np.arange(j * 256, j * 256 + 256),
        5136 + np.arange(j * 256, j * 256 + 256)])
    cols = np.concatenate([cols_fm, cols_tm])
    colT = lambda v: f(v.reshape(-1, 128).T)
    m = {}
    m["xfull"] = f(np.concatenate([inp["ctx"][b], inp["x"][b]], 0))
    xt = np.zeros((1152, D), np.float32)
    lo, hi = 1024 * j - 64, 1024 * j + 1088
    slo, shi = max(lo, 0), min(hi, LX)
    xt[slo - lo:shi - lo] = inp["x"][b, slo:shi]
    m["xtok"] = xt
    cv = np.stack([inp["c"][b], inp["c_ctx"]], 0)
    m["cT"] = f(cv.reshape(2, 16, 128).transpose(2, 1, 0))
    m["w_mod"] = f(inp["w_mod"][0])
    m["bmodT"] = f(inp["b_mod"][0].reshape(6, 16, 128).transpose(2, 0, 1))
    m["w_in_c"] = f(w_in[:, cols])
    pp = np.zeros((128, 64), np.float32)
    pp[:, 0:16] = colT(inp["g_norm1"][0]); pp[:, 16:32] = colT(inp["g_norm2"][0])
    cw, cb = inp["m_conv_w"][0], inp["m_conv_b"][0]
    qs, ks = slice(j * 128, j * 128 + 128), slice(512 + j * 128, 512 + j * 128 + 128)
    for t in range(3):
        pp[:, 32 + t] = cw[t, qs]; pp[:, 36 + t] = cw[t, ks]
    pp[:, 35] = cb[qs]; pp[:, 39] = cb[ks]
    pp[0:4, 40] = inp["m_gate_b"][0][:, j]
    pp[32:36, 40] = inp["m_gate_b"][0][:, j]
    gb = inp["m_gate_b"][0][:, j]
    pp[0, 43] = gb[0]; pp[32, 43] = gb[1]; pp[0, 44] = gb[2]; pp[32, 44] = gb[3]
    pp[:, 48 + c] = 1.0
    pp[:, 41] = 1.0 if j > 0 else 0.0
    pp[:, 42] = 1.0 if j < 3 else 0.0
    m["pp"] = pp
    rs = slice(j * 256, j * 256 + 256)
    rep = np.concatenate([inp["m_norm_g"][0][rs], inp["r_gn_w"][0][rs], inp["r_gn_b"][0][rs],
                          inp["r_w0"][0][0][rs], inp["r_w0"][0][1][rs], inp["r_a0"][0][0][rs], inp["r_a0"][0][1][rs],
                          inp["r_k_k"][0][rs], inp["r_k_a"][0][rs], inp["r_r_k"][0].reshape(-1)[rs]])
    m["rep"] = f(np.broadcast_to(rep[None, :], (128, rep.size)))
    m["gfin"] = f(np.broadcast_to(inp["g_final"][None, :], (128, D)))
    m["lora_w"] = f(inp["r_w_up"][0][:, :, rs].reshape(128, 256))
    m["lora_a"] = f(inp["r_a_up"][0][:, :, rs].reshape(128, 256))
    m["lora_g"] = f(inp["r_g_up"][0][:, rs])
    m["w_out"] = f(inp["w_out"][0])
    m["f_w_up"] = f(inp["f_w_up"][0]); m["f_w_gate"] = f(inp["f_w_gate"][0]); m["f_w_down"] = f(inp["f_w_down"][0])
    fc = np.zeros((128, NF, 10), np.float32)
    fc[:, :, 0:9] = inp["f_conv_w"][0].reshape(9, NF, 128).transpose(2, 1, 0)
    fc[:, :, 9] = inp["f_conv_b"][0].reshape(NF, 128).T
    m["fcw"] = fc
    cst = np.zeros((128, 1024), np.float32)
    cst[:, 0:128] = np.eye(128)
    ii = np.arange(128)
    cst[:, 128:256] = (ii[:, None] <= ii[None, :])
    cst[:, 256:384] = (ii[:, None] >= ii[None, :])
    cst[:, 384:512] = 1.0
    i6 = np.arange(64)
    cst[0:64, 512:576] = (i6[:, None] < i6[None, :]); cst[0:64, 576:640] = (i6[:, None] <= i6[None, :])
    cst[0:64, 640:704] = (i6[:, None] > i6[None, :]); cst[0:64, 704:768] = (i6[:, None] >= i6[None, :])
    m["cst"] = cst
    return m


class Rot:
    def __init__(self, items):
        self.items = items
        self.i = 0

    def next(self):
        t = self.items[self.i % len(self.items)]
        self.i += 1
        return t


def build(stage=99, debug=False):
    nc = bass.Bass("TRN2", target_bir_lowering=False)
    P = Prog(nc)

    def ein(name, shape, dt=F32):
        return T(nc.dram_tensor(name, list(shape), dt, kind="ExternalInput").ap(), name)

    def scratch(name, shape, dt=F32, dbg=False):
        if dbg and debug:
            return T(nc.dram_tensor(name, list(shape), dt, kind="ExternalOutput").ap(), name)
        return T(nc.dram_tensor(name, list(shape), dt).ap(), name)

    xfull = ein("xfull", [LT, D]); xtok = ein("xtok", [1152, D]); cT = ein("cT", [128, 16, 2])
    w_mod = ein("w_mod", [D, 6 * D]); bmodT = ein("bmodT", [128, 6, 16]); w_in_c = ein("w_in_c", [D, NCOL])
    pp_d = ein("pp", [128, 64]); rep_d = ein("rep", [128, 2560]); gfin_d = ein("gfin", [128, D])
    lora_w = ein("lora_w", [128, 256]); lora_a = ein("lora_a", [128, 256]); lora_g = ein("lora_g", [160, 256])
    cst_d = ein("cst", [128, 1024])
    if stage >= 4:
        w_out = ein("w_out", [D, D]); f_w_up = ein("f_w_up", [D, DFF]); f_w_gate = ein("f_w_gate", [D, DFF])
        f_w_down = ein("f_w_down", [DFF, D]); fcw_d = ein("fcw", [128, NF, 10])
    out_d = T(nc.dram_tensor("out", [1024, D], F32, kind="ExternalOutput").ap(), "out")

    FM = scratch("FM", [NFM, LT], dbg=True)
    TM = scratch("TM", [LT, NTM], dbg=True)
    MIX = scratch("MIX", [LT, 512], dbg=True)

    pp = P.sbuf("pp_s", [128, 64]); cst = P.sbuf("cst_s", [128, 1024]); cstb = P.sbuf("cstb_s", [128, 512], BF16)
    modT = P.sbuf("modT", [128, 6, 16, 2])
    drv = P.sbuf("drv", [128, 8, 16])
    P.dma(pp[:], pp_d[:], reads=[pp_d], writes=[pp])
    P.dma(cst[:], cst_d[:], reads=[cst_d], writes=[cst])
    P.op("dve", lambda e: e.tensor_copy(cstb[:], cst[:, 0:512]), reads=[cst], writes=[cstb])
    ident = cst[:, 0:128]; identb = cstb[:, 0:128]

    ps = [P.psum(f"ps{i}", [128, 512], F32) for i in range(5)]
    pb = [P.psum(f"pb{i}", [128, 1024], BF16) for i in range(2)]
    psM = P.psum("psM", [128, 512], F32)
    psr = Rot(ps[0:4])

    NROW = 64 + 2 * LX + 64
    cin_h = nc.dram_tensor("cin", [NROW, D], BF16)
    CIN = T(cin_h.ap(), "cin")
    COUT_h = nc.dram_tensor("cout", [NROW, D], BF16)
    COUT = T(COUT_h.ap(), "cout")
    zt16 = P.sbuf("zt16", [128, D], BF16)
    P.op("pool", lambda e: e.memset(zt16[:], 0.0), writes=[zt16])
    if stage >= 4:
        for i in range(NROW // 128):
            P.dma(CIN[i * 128:(i + 1) * 128, :], zt16[:], reads=[zt16], writes=[CIN])

    es0 = ExitStack()
    cTs = P.sbuf("cTs", [128, 16, 2], es=es0); scT = P.sbuf("scT", [128, 16, 2], es=es0)
    bm = P.sbuf("bm", [128, 6, 16], es=es0)
    wts = [P.sbuf(f"wmt{i}", [128, 2048], es=es0) for i in range(2)]
    P.dma(cTs[:], cT[:], reads=[cT], writes=[cTs])
    P.dma(bm[:], bmodT[:], reads=[bmodT], writes=[bm])
    P.op("act", lambda e: e.activation(scT[:], cTs[:], AF.Silu), reads=[cTs], writes=[scT])
    wrot = Rot(wts)

    def adaln(mlist):
        for m in mlist:
            for kt in range(16):
                wt = wrot.next()
                P.dma(wt[:], w_mod[kt * 128:(kt + 1) * 128, m * 2048:(m + 1) * 2048], reads=[w_mod], writes=[wt])
                for cc in range(16):
                    o = (cc * 16 + kt) * 2
                    P.mm(psM, psM[:, o:o + 2], wt, wt[:, cc * 128:(cc + 1) * 128], scT, scT[:, kt, :])
            P.op("dve", lambda e, m=m: e.tensor_reduce(modT[:, m], psM[:].rearrange("p (c k v) -> p c v k", c=16, k=16, v=2),
                                                       AX.X, ALU.add), reads=[psM], writes=[modT])
            for v in range(2):
                P.op("dve", lambda e, m=m, v=v: e.tensor_tensor(modT[:, m, :, v], modT[:, m, :, v], bm[:, m, :], ALU.add),
                     reads=[modT, bm], writes=[modT])

    adaln([0, 1])
    for v in range(2):
        P.op("dve", lambda e, v=v: e.scalar_tensor_tensor(drv[:, 2 * v, :], modT[:, 1, :, v], 1.0, pp[:, 0:16], ALU.add, ALU.mult),
             reads=[modT, pp], writes=[drv])
        P.op("dve", lambda e, v=v: e.tensor_copy(drv[:, 2 * v + 1, :], modT[:, 0, :, v]), reads=[modT], writes=[drv])

    es1 = ExitStack()
    wb = P.sbuf("wb", [128, 16, NCOL], BF16, es=es1)
    stg = [P.sbuf(f"stg{i}", [128, NCOL], es=es1) for i in range(2)]
    for kt in range(16):
        s_ = stg[kt % 2]
        P.dma(s_[:], w_in_c[kt * 128:(kt + 1) * 128, :], reads=[w_in_c], writes=[s_])
        P.op("pool", lambda e, kt=kt, s_=s_: e.tensor_copy(wb[:, kt, :], s_[:]), reads=[s_], writes=[wb])
    xts = [P.sbuf(f"xt{i}", [128, D], es=es1) for i in range(2)]
    junk = P.sbuf("junk", [128, D], BF16, es=es1)
    xn = P.sbuf("xn", [128, D], BF16, es=es1)
    st = P.sbuf("st", [128, 8], es=es1)
    hTs = [P.sbuf(f"hT{i}", [128, 16, 512], BF16, es=es1) for i in range(2)]
    tmo = [P.sbuf(f"tmo{i}", [128, NTM], es=es1) for i in range(2)]
    fmo = [P.sbuf(f"fmo{i}", [128, 512], es=es1) for i in range(3)]
    fmr = Rot(fmo)

    def norm_tile(xt, S_idx, dst, dst_off, npart=128, st_=None, xn_=None, junk_=None, ncols=None):
        st_ = st_ or st; xn_ = xn_ or xn; junk_ = junk_ or junk
        ncols = ncols or npart
        P.op("act", lambda e: e.activation(junk_[0:npart, :], xt[0:npart, :], AF.Square, accum_out=st_[0:npart, 0:1]), reads=[xt], writes=[junk_, st_])
        P.op("dve", lambda e: e.tensor_scalar(st_[0:npart, 1:2], st_[0:npart, 0:1], 1.0 / D, EPS, ALU.mult, ALU.add), reads=[st_], writes=[st_])
        P.op("act", lambda e: e.activation(st_[0:npart, 2:3], st_[0:npart, 1:2], AF.Sqrt), reads=[st_], writes=[st_])
        P.op("dve", lambda e: e.reciprocal(st_[0:npart, 3:4], st_[0:npart, 2:3]), reads=[st_], writes=[st_])
        P.op("act", lambda e: e.activation(xn_[0:npart, :], xt[0:npart, :], AF.Identity, scale=st_[0:npart, 3:4]), reads=[xt, st_], writes=[xn_])
        for half in range(2):
            pbt = pb[half]
            for k8 in range(8):
                kt = half * 8 + k8
                P.tr(pbt, pbt[:, k8 * 128:k8 * 128 + npart], xn_, xn_[0:npart, kt * 128:(kt + 1) * 128], cstb, identb[0:npart, 0:npart])
            for k8 in range(8):
                kt = half * 8 + k8
                if k8 % 2 == 0:
                    P.op("act", lambda e, kt=kt, k8=k8, pbt=pbt: e.activation(dst[:, kt, dst_off:dst_off + ncols], pbt[:, k8 * 128:k8 * 128 + ncols], AF.Identity,
                                                                          bias=drv[:, S_idx + 1, kt:kt + 1], scale=drv[:, S_idx, kt:kt + 1]),
                         reads=[pbt, drv], writes=[dst])
                else:
                    P.op("dve", lambda e, kt=kt, k8=k8, pbt=pbt: e.tensor_scalar(dst[:, kt, dst_off:dst_off + ncols], pbt[:, k8 * 128:k8 * 128 + ncols],
                                                                             drv[:, S_idx, kt:kt + 1], drv[:, S_idx + 1, kt:kt + 1], ALU.mult, ALU.add),
                         reads=[pbt, drv], writes=[dst])

    blocks = [(0, 2)] + [(2 + 4 * i, 4) for i in range(8)]
    for bi, (t0, nt) in enumerate(blocks):
        hT = hTs[bi % 2]
        ntok = nt * 128
        for ti in range(nt):
            tt = t0 + ti
            xt = xts[tt % 2]
            P.dma(xt[:], xfull[tt * 128:(tt + 1) * 128, :], reads=[xfull], writes=[xt])
            norm_tile(xt, 2 if tt < 2 else 0, hT, ti * 128)
            to = tmo[tt % 2]
            for (c0, cn) in ((0, 512), (512, 512), (1024, 256)):
                pt = psr.next()
                for kt in range(16):
                    P.mm(pt, pt[:, 0:cn], hT, hT[:, kt, ti * 128:(ti + 1) * 128], wb, wb[:, kt, NFM + c0:NFM + c0 + cn],
                         start=(kt == 0), stop=(kt == 15))
                P.op("act", lambda e, pt=pt, c0=c0, cn=cn, to=to: e.copy(to[:, c0:c0 + cn], pt[:, 0:cn]), reads=[pt], writes=[to])
            P.dma(TM[tt * 128:(tt + 1) * 128, :], to[:], reads=[to], writes=[TM])
        for (c0, cn) in FM_CH:
            pt = psr.next()
            for kt in range(16):
                P.mm(pt, pt[0:cn, 0:ntok], wb, wb[:, kt, c0:c0 + cn], hT, hT[:, kt, 0:ntok], start=(kt == 0), stop=(kt == 15))
            fo = fmr.next()
            P.op("dve", lambda e, pt=pt, cn=cn, fo=fo: e.tensor_copy(fo[0:cn, 0:ntok], pt[0:cn, 0:ntok]), reads=[pt], writes=[fo])
            P.dma(FM[c0:c0 + cn, t0 * 128:t0 * 128 + ntok], fo[0:cn, 0:ntok], reads=[fo], writes=[FM])
    P.barrier()
    es1.close()
    if stage <= 1:
        es0.close()
        P.final_wait("sp")
        P.finish()
        return nc
    def bc_ap(t_ap, n_inner):
        a = [list(x) for x in t_ap.ap]
        return bass.AP(t_ap.tensor, t_ap.offset, a + [[0, n_inner]])

    es2 = ExitStack()
    qT = P.sbuf("qT", [128, LT], BF16, es=es2); kT = P.sbuf("kT", [128, LT], BF16, es=es2)
    colS = P.sbuf("colS", [128, 5, 2, NTT], es=es2)
    es2a = ExitStack()
    raw = P.sbuf("raw", [128, LT], es=es2a); acc = P.sbuf("acc", [128, LT], es=es2a)
    for qi, (dstT, r0, wc, scl) in enumerate(((qT, 0, 32, 128 ** -0.5), (kT, 128, 36, 1.0))):
        P.dma(raw[:], FM[r0:r0 + 128, :], reads=[FM], writes=[raw])
        P.op("dve", lambda e, wc=wc: e.tensor_scalar(acc[:], raw[:], pp[:, wc + 1:wc + 2], pp[:, wc + 3:wc + 4], ALU.mult, ALU.add),
             reads=[raw, pp], writes=[acc])
        for (s0, e0) in ((0, CTX), (CTX, LT)):
            P.op("dve", lambda e, wc=wc, s0=s0, e0=e0: e.scalar_tensor_tensor(acc[:, s0 + 1:e0], raw[:, s0:e0 - 1], pp[:, wc:wc + 1], acc[:, s0 + 1:e0], ALU.mult, ALU.add),
                 reads=[raw, pp, acc], writes=[acc])
            P.op("dve", lambda e, wc=wc, s0=s0, e0=e0: e.scalar_tensor_tensor(acc[:, s0:e0 - 1], raw[:, s0 + 1:e0], pp[:, wc + 2:wc + 3], acc[:, s0:e0 - 1], ALU.mult, ALU.add),
                 reads=[raw, pp, acc], writes=[acc])
        P.op("act", lambda e, dstT=dstT, scl=scl: e.activation(dstT[:], acc[:], AF.Copy, scale=scl), reads=[acc], writes=[dstT])
    P.barrier()
    es2a.close()
    if stage == 2.1:
        dq = scratch("DQ", [128, LT], BF16, dbg=True)
        P.dma(dq[:], qT[:], reads=[qT], writes=[dq])
        P.final_wait("sp"); P.barrier(); es2.close(); es0.close(); P.finish(); return nc
    es2b = ExitStack()
    R = {n: P.sbuf("row_" + n, [33, LT], es=es2b) for n in ("I", "F", "Bg", "A", "Mt", "Z")}
    Mend = P.sbuf("Mend", [33, NTT], es=es2b); Mst = P.sbuf("Mst", [33, NTT], es=es2b); E5 = P.sbuf("E5", [33, NTT], es=es2b)
    for n in ("I", "F"):
        P.op("pool", lambda e, n=n: e.memset(R[n][:], 0.0), writes=[R[n]])
    P.op("pool", lambda e: e.memset(R["Z"][:], 0.0), writes=[R["Z"]])
    P.op("pool", lambda e: e.memset(Mst[:], 0.0), writes=[Mst])
    P.dma(R["I"][0:1, :], FM[928:929, :], reads=[FM], writes=[R["I"]])
    P.dma(R["I"][32:33, :], FM[929:930, :], reads=[FM], writes=[R["I"]])
    P.dma(R["F"][0:1, :], FM[930:931, :], reads=[FM], writes=[R["F"]])
    P.dma(R["F"][32:33, :], FM[931:932, :], reads=[FM], writes=[R["F"]])
    I_, F_, Bg, A_, Mt, Z_ = (R[n] for n in ("I", "F", "Bg", "A", "Mt", "Z"))
    P.op("dve", lambda e: e.tensor_scalar(I_[:], I_[:], pp[0:33, 43:44], None, ALU.add), reads=[I_, pp], writes=[I_])
    P.op("dve", lambda e: e.tensor_scalar(F_[:], F_[:], pp[0:33, 44:45], None, ALU.add), reads=[F_, pp], writes=[F_])
    P.op("act", lambda e: e.activation(F_[:], F_[:], AF.Exp, scale=-1.0), reads=[F_], writes=[F_])
    P.op("act", lambda e: e.activation(F_[:], F_[:], AF.Ln, bias=1.0), reads=[F_], writes=[F_])
    P.op("dve", lambda e: e.tensor_scalar(F_[:], F_[:], -1.0, None, ALU.mult), reads=[F_], writes=[F_])

    def scan_dirs(out, d0, d1, op0, op1, fix):
        P.op("dve", lambda e: e.tensor_tensor_scan(out[0:1, :], d0[0:1, :], d1[0:1, :], 0.0, op0, op1), reads=[d0, d1], writes=[out])
        P.op("dve", lambda e: e.tensor_tensor_scan(out[32:33, CTX - 1::-1], d0[32:33, CTX - 1::-1], d1[32:33, CTX - 1::-1], 0.0, op0, op1),
             reads=[d0, d1], writes=[out])
        P.op("dve", lambda e: e.tensor_tensor_scan(out[32:33, LT - 1:CTX - 1:-1], d0[32:33, LT - 1:CTX - 1:-1], d1[32:33, LT - 1:CTX - 1:-1],
                                                   0.0, op0, op1), reads=[d0, d1, out], writes=[out])
        P.op("dve", lambda e: e.tensor_scalar(out[32:33, CTX:LT], out[32:33, CTX:LT], out[32:33, 0:1], None, fix), reads=[out], writes=[out])

    scan_dirs(Bg, F_, Z_, ALU.add, ALU.add, ALU.add)
    P.op("dve", lambda e: e.tensor_tensor(A_[:], I_[:], Bg[:], ALU.subtract), reads=[I_, Bg], writes=[A_])
    scan_dirs(Mt, A_, A_, ALU.max, ALU.max, ALU.max)
    P.op("dve", lambda e: e.tensor_copy(Mend[0:1, :], Mt[0:1, 127::128]), reads=[Mt], writes=[Mend])
    P.op("dve", lambda e: e.tensor_copy(Mst[0:1, 1:NTT], Mt[0:1, 127:LT - 128:128]), reads=[Mt], writes=[Mst])
    P.op("dve", lambda e: e.tensor_copy(Mend[32:33, :], Mt[32:33, 0::128]), reads=[Mt], writes=[Mend])
    P.op("dve", lambda e: e.tensor_copy(Mst[32:33, 2:NTT - 1], Mt[32:33, 384::128]), reads=[Mt], writes=[Mst])
    P.op("dve", lambda e: e.tensor_copy(Mst[32:33, NTT - 1:NTT], Mt[32:33, 0:1]), reads=[Mt], writes=[Mst])
    P.op("dve", lambda e: e.tensor_copy(Mst[32:33, 0:1], Mt[32:33, 128:129]), reads=[Mt], writes=[Mst])
    v3 = lambda t: t[:].rearrange("p (c k) -> p c k", k=128)
    E1, E2, E3, E4 = I_, F_, Z_, Bg
    P.op("dve", lambda e: e.tensor_tensor(v3(E1), v3(A_), bc_ap(Mend[:], 128), ALU.subtract), reads=[A_, Mend], writes=[E1])
    P.op("dve", lambda e: e.tensor_tensor(v3(E2), bc_ap(Mend[:], 128), v3(Mt), ALU.subtract), reads=[Mt, Mend], writes=[E2])
    P.op("dve", lambda e: e.tensor_tensor(v3(E3), bc_ap(Mst[:], 128), v3(Mt), ALU.subtract), reads=[Mt, Mst], writes=[E3])
    P.op("dve", lambda e: e.tensor_tensor(E4[:], Bg[:], Mt[:], ALU.add), reads=[Bg, Mt], writes=[E4])
    P.op("dve", lambda e: e.tensor_tensor(E5[:], Mst[:], Mend[:], ALU.subtract), reads=[Mst, Mend], writes=[E5])
    for Ei in (E1, E2, E3):
        P.op("act", lambda e, Ei=Ei: e.activation(Ei[:], Ei[:], AF.Exp), reads=[Ei], writes=[Ei])
    P.op("act", lambda e: e.activation(E4[:], E4[:], AF.Exp, scale=-1.0), reads=[E4], writes=[E4])
    P.op("act", lambda e: e.activation(E5[:], E5[:], AF.Exp), reads=[E5], writes=[E5])
    P.op("dve", lambda e: e.tensor_copy(v3(A_), bc_ap(E5[:], 128)), reads=[E5], writes=[A_])
    pc = ps[4]
    for qi, Ei in enumerate((E1, E2, E3, E4, A_)):
        for g0 in range(0, NTT, 12):
            gn = min(12, NTT - g0)
            for ti in range(gn):
                tt = g0 + ti
                P.tr(pc, pc[:, ti * 33:(ti + 1) * 33], Ei, Ei[0:33, tt * 128:(tt + 1) * 128], cst, cst[0:33, 0:33])
            for d, c0 in enumerate((0, 32)):
                P.op("dve", lambda e, qi=qi, d=d, c0=c0, g0=g0, gn=gn: e.tensor_copy(colS[:, qi, d, g0:g0 + gn], pc[:, c0:c0 + 33 * gn:33]),
                     reads=[pc], writes=[colS])
    P.barrier()
    es2b.close()
    if stage == 2.2:
        dq = scratch("DCS", [128, 10 * NTT], F32, dbg=True)
        P.dma(dq[:], colS[:].rearrange("p a b c -> p (a b c)"), reads=[colS], writes=[dq])
        P.final_wait("sp"); P.barrier(); es2.close(); es0.close(); P.finish(); return nc
    es2c = ExitStack()
    HF = P.sbuf("HF", [128, 32, 256], es=es2c)
    CTs = P.sbuf("CTs", [128, 257], es=es2c); CTb = P.sbuf("CTb", [128, 257], BF16, es=es2c)
    vts = [P.sbuf(f"vt{i}", [128, 256], es=es2c) for i in range(2)]
    oms = [P.sbuf(f"om{i}", [128, 256], es=es2c) for i in range(2)]
    M2 = [dict(vw=P.sbuf(f"vw{k}", [128, 257], BF16, es=es2c), ktm=P.sbuf(f"ktm{k}", [128, 128], BF16, es=es2c),
               PT=P.sbuf(f"PT{k}", [128, 128], BF16, es=es2c), tmp=P.sbuf(f"tmp{k}", [128, 257], es=es2c), num=P.sbuf(f"num{k}", [128, 257], es=es2c),
               sm=P.sbuf(f"sm{k}", [128, 8], es=es2c), hcur=P.sbuf(f"hcur{k}", [128, 256], es=es2c), hout=P.sbuf(f"hout{k}", [128, 256], es=es2c),
               jk2=P.sbuf(f"jk2{k}", [128, 256], es=es2c)) for k in range(2)]
    repm = P.sbuf("repm", [128, 256], es=es2c)
    P.dma(repm[:], rep_d[:, 0:256], reads=[rep_d], writes=[repm])
    bk2 = Rot(ps[0:5] + [psM])
    for d in range(2):
        P.op("pool", lambda e: e.memset(CTs[:], 0.0), writes=[CTs])
        P.op("pool", lambda e: e.memset(CTb[:], 0.0), writes=[CTb])
        order = list(range(NTT)) if d == 0 else [1, 0] + list(range(NTT - 1, 1, -1))
        mask = cst[:, 128:256] if d == 0 else cst[:, 256:384]
        for ci, c in enumerate(order):
            tok = slice(c * 128, (c + 1) * 128)
            isx = c >= 2
            vt = vts[ci % 2]
            m2 = M2[ci % 2]
            q0_, q1_, q2_, q3_ = bk2.next(), bk2.next(), bk2.next(), bk2.next()
            pbx_ = pb[ci % 2]
            vw, ktm, PT, tmp, num, sm, hcur, hout, jk2 = (m2[n] for n in ("vw", "ktm", "PT", "tmp", "num", "sm", "hcur", "hout", "jk2"))
            P.dma(vt[:], TM[tok, 0:256], reads=[TM], writes=[vt])
            wk = colS[:, 0, d, c:c + 1]; eb = colS[:, 1, d, c:c + 1]; inter = colS[:, 2, d, c:c + 1]; thr = colS[:, 3, d, c:c + 1]
            dec = colS[:, 4, d, c:c + 1]
            P.op("dve", lambda e, vt=vt, wk=wk: e.tensor_scalar(vw[:, 0:256], vt[:], wk, None, ALU.mult), reads=[vt, colS], writes=[vw])
            P.op("act", lambda e, wk=wk: e.copy(vw[:, 256:257], wk), reads=[colS], writes=[vw])
            P.tr(pbx_, pbx_[:, 0:128], kT, kT[:, tok], cstb, identb)
            P.op("act", lambda e: e.copy(ktm[:], pbx_[:, 0:128]), reads=[pbx_], writes=[ktm])
            if isx:
                P.mm(q0_, q0_[:, 0:128], kT, kT[:, tok], qT, qT[:, tok])
                P.op("dve", lambda e, mask=mask: e.tensor_tensor(PT[:], q0_[:, 0:128], mask, ALU.mult), reads=[q0_, cst], writes=[PT])
                P.mm(q1_, q1_[:, 0:257], PT, PT[:], vw, vw[:])
                P.mm(q2_, q2_[:, 0:257], qT, qT[:, tok], CTb, CTb[:])
                P.op("dve", lambda e, inter=inter: e.tensor_scalar(tmp[:], q2_[:, 0:257], inter, None, ALU.mult), reads=[q2_, colS], writes=[tmp])
                P.op("dve", lambda e, eb=eb: e.scalar_tensor_tensor(num[:], q1_[:, 0:257], eb, tmp[:], ALU.mult, ALU.add), reads=[q1_, colS, tmp], writes=[num])
                P.op("dve", lambda e, thr=thr: e.tensor_scalar(sm[:, 0:1], num[:, 256:257], -1.0, thr, ALU.mult, ALU.max), reads=[num, colS], writes=[sm])
                P.op("dve", lambda e: e.tensor_tensor(sm[:, 1:2], sm[:, 0:1], num[:, 256:257], ALU.max), reads=[sm, num], writes=[sm])
                P.op("dve", lambda e: e.reciprocal(sm[:, 2:3], sm[:, 1:2]), reads=[sm], writes=[sm])
                if d == 0:
                    P.op("dve", lambda e, c=c: e.tensor_scalar(HF[:, c - 2, :], num[:, 0:256], sm[:, 2:3], None, ALU.mult), reads=[num, sm], writes=[HF])
                else:
                    om = oms[ci % 2]
                    P.dma(om[:], TM[tok, 256:512], reads=[TM], writes=[om])
                    P.op("dve", lambda e, c=c: e.scalar_tensor_tensor(hcur[:], num[:, 0:256], sm[:, 2:3], HF[:, c - 2, :], ALU.mult, ALU.add),
                         reads=[num, sm, HF], writes=[hcur])
                    P.op("act", lambda e: e.activation(jk2[:], hcur[:], AF.Square, accum_out=sm[:, 3:4]), reads=[hcur], writes=[jk2, sm])
                    P.op("dve", lambda e: e.tensor_scalar(sm[:, 4:5], sm[:, 3:4], 1.0 / 256, EPS, ALU.mult, ALU.add), reads=[sm], writes=[sm])
                    P.op("act", lambda e: e.activation(sm[:, 5:6], sm[:, 4:5], AF.Sqrt), reads=[sm], writes=[sm])
                    P.op("dve", lambda e: e.reciprocal(sm[:, 6:7], sm[:, 5:6]), reads=[sm], writes=[sm])
                    P.op("act", lambda e, om=om: e.activation(om[:], om[:], AF.Sigmoid), reads=[om], writes=[om])
                    P.op("dve", lambda e: e.scalar_tensor_tensor(hout[:], hcur[:], sm[:, 6:7], repm[:], ALU.mult, ALU.mult), reads=[hcur, sm, repm], writes=[hout])
                    P.op("dve", lambda e, om=om: e.tensor_tensor(hout[:], hout[:], om[:], ALU.mult), reads=[hout, om], writes=[hout])
                    P.dma(MIX[tok, 0:256], hout[:], reads=[hout], writes=[MIX])
            P.mm(q3_, q3_[:, 0:257], ktm, ktm[:], vw, vw[:])
            P.op("dve", lambda e, dec=dec: e.scalar_tensor_tensor(CTs[:], CTs[:], dec, q3_[:, 0:257], ALU.mult, ALU.add), reads=[CTs, colS, q3_], writes=[CTs])
            P.op("act", lambda e: e.copy(CTb[:], CTs[:]), reads=[CTs], writes=[CTb])
    P.barrier()
    es2c.close()
    es2.close()
    if stage <= 2:
        es0.close()
        P.final_wait("sp")
        P.finish()
        return nc
    def bc_mid(t_ap, n):
        a = [list(x) for x in t_ap.ap]
        return bass.AP(t_ap.tensor, t_ap.offset, [a[0], [0, n]] + a[1:])

    YF = scratch("YF", [LT, 256])
    es3 = ExitStack()
    NCH = LT // 64
    banks = Rot(ps[0:5] + [psM])
    repr_ = P.sbuf("repr", [64, 2304], es=es3)
    P.dma(repr_[:], rep_d[0:64, 256:2560], reads=[rep_d], writes=[repr_])
    gnw, gnb = repr_[:, 0:256], repr_[:, 256:512]
    w0r = (repr_[:, 512:768], repr_[:, 768:1024]); a0r = (repr_[:, 1024:1280], repr_[:, 1280:1536])
    rkk, rka, rrk = repr_[:, 1536:1792], repr_[:, 1792:2048], repr_[:, 2048:2304]
    def sb(n, sh, dt=F32, rows=64):
        full = es3.enter_context(nc.sbuf_tensor(n, [128] + list(sh[1:]), dt))
        t = T(full[0:rows], n)
        t.full = full
        P.op("pool", lambda e: e.memset(full[:], 0.0), writes=[t])
        return t
    lwTs = [sb(f"lwT{k}", [64, LT]) for k in range(2)]; laTs = [sb(f"laT{k}", [64, LT]) for k in range(2)]
    lgs = P.sbuf("lgs", [128, LT // 4], es=es3); sgb0 = sb("sgb0", [128, LT], BF16, rows=128); sgb1 = sb("sgb1", [32, LT], BF16, rows=32)
    lwds = [sb(f"lwd{k}", [64, 256]) for k in range(2)]; lads = [sb(f"lad{k}", [64, 256]) for k in range(2)]
    gu = P.sbuf("gu", [128, 256], es=es3); gub0 = sb("gub0", [128, 256], BF16, rows=128); gub1 = sb("gub1", [32, 256], BF16, rows=32)
    omk = P.sbuf("omk", [64, 256], es=es3)
    P.op("dve", lambda e: e.tensor_scalar(omk[:], rka, -1.0, 1.0, ALU.mult, ALU.add), reads=[repr_], writes=[omk])
    rkvs = [sb(f"rkv{i}", [64, 768]) for i in range(3)]
    SET_NAMES = ["zt", "logw", "at", "kk", "sq", "kka", "kd", "tt_", "cums", "ex", "s4", "Ahb", "Rhb", "Btb", "Ktb", "Kbar", "Bbar", "Vb", "WCc",
                 "FMo", "NQ", "GH", "MZ0", "MZ1", "Np0", "Np1", "Xneg", "RpT", "Phi", "Y0s", "PsiT", "ycur", "yfl", "cen", "af", "outt"]

    def alloc_set(k):
        d_ = {}
        for n in ("zt", "logw", "at", "kk", "sq", "kka", "kd", "tt_", "cums", "ex", "ycur", "yfl", "cen", "af", "outt"):
            d_[n] = sb(f"{n}_{k}", [64, 256])
        for n in ("Ahb", "Rhb", "Btb", "Ktb", "Kbar", "Bbar", "Vb"):
            d_[n] = sb(f"{n}_{k}", [64, 256], BF16)
        d_["s4"] = sb(f"s4_{k}", [64, 16]); d_["WCc"] = sb(f"WCc_{k}", [64, 8])
        d_["FMo"] = sb(f"FMo_{k}", [64, 4, 4, 64], BF16)
        d_["NQ"] = sb(f"NQ_{k}", [64, 4, 128], BF16); d_["GH"] = sb(f"GH_{k}", [64, 4, 128], BF16)
        d_["MZ0"] = sb(f"MZ0_{k}", [64, 4, 192], BF16); d_["MZ1"] = sb(f"MZ1_{k}", [64, 4, 192], BF16)
        d_["Np0"] = sb(f"Np0_{k}", [64, 4, 64], BF16); d_["Np1"] = sb(f"Np1_{k}", [64, 4, 64], BF16)
        for n in ("Xneg", "RpT", "Phi"):
            d_[n] = sb(f"{n}_{k}", [64, 4, 64], BF16)
        for n in ("Y0s", "PsiT"):
            d_[n] = sb(f"{n}_{k}", [64, 4, 64])
        return d_

    TSETS = [alloc_set(0), alloc_set(1)]
    STss = [sb(f"STs{k}", [64, 4, 64]) for k in range(2)]; STbs = [sb(f"STb{k}", [64, 4, 64], BF16) for k in range(2)]
    YB = scratch("YB", [LT, 256])
    h4 = lambda t: t[:].rearrange("p (h n) -> p h n", n=64)
    ident64 = cst[0:64, 0:64]
    LQ = LT // 4
    for q_ in range(4):
        P.dma(lgs[:], FM[512:640, q_ * LQ:(q_ + 1) * LQ], reads=[FM], writes=[lgs])
        P.op("act", lambda e: e.activation(sgb0[:, q_ * LQ:(q_ + 1) * LQ], lgs[:], AF.Sigmoid), reads=[lgs], writes=[sgb0])
    for q_ in range(4):
        P.dma(lgs[0:32, :], FM[640:672, q_ * LQ:(q_ + 1) * LQ], reads=[FM], writes=[lgs])
        P.op("act", lambda e: e.activation(sgb1[:, q_ * LQ:(q_ + 1) * LQ], lgs[0:32, :], AF.Sigmoid), reads=[lgs], writes=[sgb1])
    P.dma(gu[:], lora_g[0:128, :], reads=[lora_g], writes=[gu])
    P.op("dve", lambda e: e.tensor_copy(gub0[:], gu[:]), reads=[gu], writes=[gub0])
    P.dma(gu[0:32, :], lora_g[128:160, :], reads=[lora_g], writes=[gu])
    P.op("dve", lambda e: e.tensor_copy(gub1[:], gu[0:32, :]), reads=[gu], writes=[gub1])

    def run_pass(d):
        lwT, laT, lwd, lad = lwTs[d], laTs[d], lwds[d], lads[d]
        STs, STb = STss[d], STbs[d]
        P.dma(lwT[:], FM[256 + 64 * d:320 + 64 * d, :], reads=[FM], writes=[lwT])
        P.op("act", lambda e: e.activation(lwT[:], lwT[:], AF.Tanh), reads=[lwT], writes=[lwT])
        P.dma(laT[:], FM[384 + 64 * d:448 + 64 * d, :], reads=[FM], writes=[laT])
        P.dma(lwd[:], lora_w[64 * d:64 * d + 64, :], reads=[lora_w], writes=[lwd])
        P.dma(lad[:], lora_a[64 * d:64 * d + 64, :], reads=[lora_a], writes=[lad])
        yield
        order = list(range(NCH)) if d == 0 else [3, 2, 1, 0] + list(range(NCH - 1, 3, -1))
        tri = cst[:, 576:640] if d == 0 else cst[:, 704:768]
        mNQ = cst[0:64, 512:640] if d == 0 else cst[0:64, 640:768]
        mM = cst[0:64, 640:704] if d == 0 else cst[0:64, 512:576]
        for ci, c in enumerate(order):
            tok = slice(c * 64, (c + 1) * 64)
            isx = c >= 4
            rkv = rkvs[d]
            (zt, logw, at, kk, sq, kka, kd, tt_, cums, ex, s4, Ahb, Rhb, Btb, Ktb, Kbar, Bbar, Vb, WCc,
             FMo, NQ, GH, MZ0, MZ1, Np0, Np1, Xneg, RpT, Phi, Y0s, PsiT, ycur, yfl, cen, af, outt) = [TSETS[d][n] for n in SET_NAMES]
            MZ = [MZ0, MZ1]; Np = [Np0, Np1]
            P.dma(rkv[:], TM[tok, 512:1280], reads=[TM], writes=[rkv])
            r_, kr_, v_ = rkv[:, 0:256], rkv[:, 256:512], rkv[:, 512:768]
            b1 = banks.next()
            P.mm(b1, b1[0:64, 0:256], lwT, lwT.full[:, tok], lwd, lwd.full[:])
            P.mm(b1, b1[0:64, 256:512], laT, laT.full[:, tok], lad, lad.full[:])
            P.op("dve", lambda e, b1=b1, d=d: e.tensor_tensor(zt[:], b1[0:64, 0:256], w0r[d], ALU.add), reads=[b1, repr_], writes=[zt])
            P.op("act", lambda e: e.activation(zt[:], zt[:], AF.Sigmoid), reads=[zt], writes=[zt])
            P.op("dve", lambda e: e.tensor_scalar(logw[:], zt[:], -0.6065306597126334, None, ALU.mult), reads=[zt], writes=[logw])
            P.op("dve", lambda e, b1=b1, d=d: e.tensor_tensor(at[:], b1[0:64, 256:512], a0r[d], ALU.add), reads=[b1, repr_], writes=[at])
            P.op("act", lambda e: e.activation(at[:], at[:], AF.Sigmoid), reads=[at], writes=[at])
            P.op("dve", lambda e, kr_=kr_: e.tensor_tensor(kk[:], kr_, rkk, ALU.mult), reads=[rkv, repr_], writes=[kk])
            P.op("dve", lambda e: e.tensor_tensor(sq[:], kk[:], kk[:], ALU.mult), reads=[kk], writes=[sq])
            P.op("dve", lambda e: e.tensor_reduce(s4[:, 0:4], h4(sq), AX.X, ALU.add), reads=[sq], writes=[s4])
            P.op("dve", lambda e: e.tensor_scalar(s4[:, 0:4], s4[:, 0:4], 1e-12, None, ALU.add), reads=[s4], writes=[s4])
            P.op("act", lambda e: e.activation(s4[:, 0:4], s4[:, 0:4], AF.Sqrt), reads=[s4], writes=[s4])
            P.op("dve", lambda e: e.reciprocal(s4[:, 4:8], s4[:, 0:4]), reads=[s4], writes=[s4])
            P.op("dve", lambda e: e.tensor_tensor(h4(kk), h4(kk), bc_ap(s4[:, 4:8], 64), ALU.mult), reads=[kk, s4], writes=[kk])
            P.op("dve", lambda e: e.tensor_tensor(kka[:], kk[:], at[:], ALU.mult), reads=[kk, at], writes=[kka])
            P.op("dve", lambda e: e.tensor_tensor(tt_[:], at[:], rka, ALU.mult), reads=[at, repr_], writes=[tt_])
            P.op("dve", lambda e: e.tensor_tensor(tt_[:], tt_[:], omk[:], ALU.add), reads=[tt_, omk], writes=[tt_])
            P.op("dve", lambda e, kr_=kr_: e.tensor_tensor(kd[:], kr_, tt_[:], ALU.mult), reads=[rkv, tt_], writes=[kd])
            yield
            b2 = banks.next()
            P.mm(b2, b2[0:64, 0:256], cst, tri, logw, logw.full[:])
            P.mm(b2, b2[0:64, 256:512], cst, cst[:, 384:448], logw, logw.full[:])
            P.op("act", lambda e, b2=b2: e.copy(cums[:], b2[0:64, 0:256]), reads=[b2], writes=[cums])
            P.op("act", lambda e: e.activation(ex[:], cums[:], AF.Exp), reads=[cums], writes=[ex])
            P.op("dve", lambda e, r_=r_: e.tensor_tensor(Rhb[:], r_, ex[:], ALU.mult), reads=[rkv, ex], writes=[Rhb])
            P.op("dve", lambda e: e.tensor_tensor(ex[:], cums[:], logw[:], ALU.subtract), reads=[cums, logw], writes=[ex])
            P.op("act", lambda e: e.activation(ex[:], ex[:], AF.Exp), reads=[ex], writes=[ex])
            P.op("dve", lambda e: e.tensor_tensor(Ahb[:], kk[:], ex[:], ALU.mult), reads=[kk, ex], writes=[Ahb])
            P.op("act", lambda e: e.activation(ex[:], cums[:], AF.Exp, scale=-1.0), reads=[cums], writes=[ex])
            P.op("dve", lambda e: e.tensor_tensor(Ktb[:], kd[:], ex[:], ALU.mult), reads=[kd, ex], writes=[Ktb])
            P.op("dve", lambda e: e.tensor_tensor(Btb[:], kka[:], ex[:], ALU.mult), reads=[kka, ex], writes=[Btb])
            P.op("dve", lambda e, b2=b2: e.tensor_tensor(ex[:], b2[0:64, 256:512], cums[:], ALU.subtract), reads=[b2, cums], writes=[ex])
            P.op("act", lambda e: e.activation(ex[:], ex[:], AF.Exp), reads=[ex], writes=[ex])
            P.op("dve", lambda e: e.tensor_tensor(Kbar[:], kd[:], ex[:], ALU.mult), reads=[kd, ex], writes=[Kbar])
            P.op("dve", lambda e: e.tensor_tensor(Bbar[:], kka[:], ex[:], ALU.mult), reads=[kka, ex], writes=[Bbar])
            P.op("act", lambda e, v_=v_: e.copy(Vb[:], v_), reads=[rkv], writes=[Vb])
            yield
            b3 = banks.next()
            for h in range(4):
                P.mm(b3, b3[0:64, 2 * h:2 * h + 2], logw, logw.full[:, h * 64:(h + 1) * 64], cst, cst[:, 384:386])
            P.op("act", lambda e, b3=b3: e.activation(WCc[:], b3[0:64, 0:8], AF.Exp), reads=[b3], writes=[WCc])
            for h in range(4):
                for qi, Q in enumerate((Ahb, Rhb, Btb, Ktb)):
                    blk = h * 4 + qi
                    pbt = pb[blk // 8]; o = (blk % 8) * 128
                    P.tr(pbt, pbt[0:64, o:o + 128], Q, Q.full[:, h * 64:(h + 1) * 64], cstb, identb)
            for ti in range(2):
                P.op("act", lambda e, ti=ti: e.copy(FMo[:, 2 * ti:2 * ti + 2, :, :].rearrange("p a b c -> p (a b) c"),
                                                    pb[ti][0:64, :].rearrange("p (b c) -> p b c", c=128)[:, :, 0:64]), reads=[pb[ti]], writes=[FMo])
            ARh = lambda h: FMo.full[:, h, 0:2, :].rearrange("p a b -> p (a b)")
            bNQ, bGH, bMG = banks.next(), banks.next(), banks.next()
            for h in range(4):
                P.mm(bNQ, bNQ[0:64, h * 128:(h + 1) * 128], FMo, FMo.full[:, h, 2, :], FMo, ARh(h))
                P.mm(bGH, bGH[0:64, h * 128:(h + 1) * 128], FMo, FMo.full[:, h, 3, :], FMo, ARh(h))
                P.mm(bMG, bMG[0:64, h * 64:(h + 1) * 64], FMo, FMo.full[:, h, 0, :], FMo, FMo.full[:, h, 2, :])
            v3p = lambda b, n: b[0:64, 0:4 * n].rearrange("p (h n) -> p h n", n=n)
            P.op("dve", lambda e, bNQ=bNQ, mNQ=mNQ: e.tensor_tensor(NQ[:], v3p(bNQ, 128), bc_mid(mNQ, 4), ALU.mult), reads=[bNQ, cst], writes=[NQ])
            P.op("dve", lambda e, bGH=bGH, mNQ=mNQ: e.tensor_tensor(GH[:], v3p(bGH, 128), bc_mid(mNQ, 4), ALU.mult), reads=[bGH, cst], writes=[GH])
            yield
            cur = 0
            P.op("dve", lambda e, bMG=bMG, mM=mM: e.tensor_tensor(MZ[0][:, :, 0:64], v3p(bMG, 64), bc_mid(mM, 4), ALU.mult), reads=[bMG, cst], writes=[MZ[0]])
            for h in range(4):
                P.mm(bMG, bMG[0:64, 256 + h * 64:256 + (h + 1) * 64], GH, GH.full[:, h, 0:64], Vb, Vb.full[:, h * 64:(h + 1) * 64])
            P.op("act", lambda e: e.copy(MZ[0][:, :, 64:128], h4(Ahb)), reads=[Ahb], writes=[MZ[0]])
            P.op("act", lambda e, bMG=bMG: e.copy(MZ[0][:, :, 128:192], bMG[0:64, 256:512].rearrange("p (h n) -> p h n", n=64)), reads=[bMG], writes=[MZ[0]])
            for lvl in range(6):
                mz, mzn = MZ[cur], MZ[1 - cur]
                bA, bB = banks.next(), banks.next()
                for h in range(4):
                    bb = bA if h < 2 else bB
                    o = (h % 2) * 192
                    if lvl == 0:
                        P.mm(bb, bb[0:64, o:o + 192], NQ, NQ.full[:, h, 0:64], mz, mz.full[:, h, :])
                    else:
                        P.mm(bb, bb[0:64, o:o + 192], Np[cur], Np[cur].full[:, h, :], mz, mz.full[:, h, :])
                for hb, bb in enumerate((bA, bB)):
                    pv = bb[0:64, 0:384].rearrange("p (h n) -> p h n", n=192)
                    P.op("dve", lambda e, pv=pv, hb=hb, mz=mz, mzn=mzn, lvl=lvl: e.tensor_tensor(
                        mzn[:, 2 * hb:2 * hb + 2, 64:192], mz[:, 2 * hb:2 * hb + 2, 64:192], pv[:, :, 64:192],
                        ALU.subtract if lvl == 0 else ALU.add), reads=[bb, mz], writes=[mzn])
                    if lvl < 5:
                        P.op("act", lambda e, pv=pv, hb=hb, mzn=mzn: e.copy(mzn[:, 2 * hb:2 * hb + 2, 0:64], pv[:, :, 0:64]), reads=[bb], writes=[mzn])
                if lvl < 5:
                    bC = banks.next()
                    for h in range(4):
                        if lvl == 0:
                            P.mm(bC, bC[0:64, h * 64:(h + 1) * 64], mz, mz.full[:, h, 0:64], NQ, NQ.full[:, h, 0:64])
                        else:
                            P.mm(bC, bC[0:64, h * 64:(h + 1) * 64], mz, mz.full[:, h, 0:64], Np[cur], Np[cur].full[:, h, :])
                    P.op("act", lambda e, bC=bC, cur=cur: e.copy(Np[1 - cur][:], v3p(bC, 64)), reads=[bC], writes=[Np[1 - cur]])
                cur = 1 - cur
                yield
            Zf = MZ[cur]
            P.op("act", lambda e, Zf=Zf: e.activation(Xneg[:], Zf[:, :, 128:192], AF.Copy, scale=-1.0), reads=[Zf], writes=[Xneg])
            bR, bY, bP, bS = banks.next(), banks.next(), banks.next(), banks.next()
            for h in range(4):
                hs = slice(h * 64, (h + 1) * 64)
                P.mm(bR, bR[0:64, hs], Zf, Zf.full[:, h, 64:128], NQ, NQ.full[:, h, 64:128])
                P.mm(bY, bY[0:64, hs], GH, GH.full[:, h, 64:128], Vb, Vb.full[:, hs], start=True, stop=False)
                P.mm(bY, bY[0:64, hs], NQ, NQ.full[:, h, 64:128], Xneg, Xneg.full[:, h, :], start=False, stop=True)
                P.mm(bP, bP[0:64, hs], Zf, Zf.full[:, h, 64:128], Bbar, Bbar.full[:, hs])
                P.mm(bS, bS[0:64, hs], Kbar, Kbar.full[:, hs], Vb, Vb.full[:, hs], start=True, stop=False)
                P.mm(bS, bS[0:64, hs], Bbar, Bbar.full[:, hs], Xneg, Xneg.full[:, h, :], start=False, stop=True)
            P.op("dve", lambda e, bR=bR: e.tensor_tensor(RpT[:], FMo[:, :, 1, :], v3p(bR, 64), ALU.subtract), reads=[FMo, bR], writes=[RpT])
            P.op("act", lambda e, bY=bY: e.copy(Y0s[:], v3p(bY, 64)), reads=[bY], writes=[Y0s])
            for h in range(4):
                P.op("dve", lambda e, h=h, bP=bP: e.scalar_tensor_tensor(Phi[:, h, :], ident64, WCc[:, 2 * h:2 * h + 1], bP[0:64, h * 64:(h + 1) * 64],
                                                                     ALU.mult, ALU.subtract), reads=[cst, WCc, bP], writes=[Phi])
            P.op("act", lambda e, bS=bS: e.copy(PsiT[:], v3p(bS, 64)), reads=[bS], writes=[PsiT])
            yield
            bQ = banks.next()
            for h in range(4):
                hs = slice(h * 64, (h + 1) * 64)
                if isx:
                    P.mm(bQ, bQ[0:64, hs], RpT, RpT.full[:, h, :], STb, STb.full[:, h, :])
                P.mm(bQ, bQ[0:64, 256 + h * 64:256 + (h + 1) * 64], Phi, Phi.full[:, h, :], STb, STb.full[:, h, :])
            if isx:
                P.op("dve", lambda e, bQ=bQ: e.tensor_tensor(h4(ycur), v3p(bQ, 64), Y0s[:], ALU.add), reads=[bQ, Y0s], writes=[ycur])
            P.op("dve", lambda e, bQ=bQ: e.tensor_tensor(STs[:], bQ[0:64, 256:512].rearrange("p (h n) -> p h n", n=64), PsiT[:], ALU.add),
                 reads=[bQ, PsiT], writes=[STs])
            P.op("act", lambda e: e.copy(STb[:], STs[:]), reads=[STs], writes=[STb])
            if debug and d == 0 and c in (0, 1):
                f32l = [("logw", logw, logw[:], 256), ("at", at, at[:], 256), ("kk", kk, kk[:], 256), ("kd", kd, kd[:], 256), ("cums", cums, cums[:], 256),
                        ("WCc", WCc, WCc[:], 8), ("Y0s", Y0s, Y0s[:].rearrange("p a b -> p (a b)"), 256), ("PsiT", PsiT, PsiT[:].rearrange("p a b -> p (a b)"), 256),
                        ("STs", STs, STs[:].rearrange("p a b -> p (a b)"), 256)]
                bfl = [("Ahb", Ahb, Ahb[:], 256), ("Rhb", Rhb, Rhb[:], 256), ("Ktb", Ktb, Ktb[:], 256), ("Btb", Btb, Btb[:], 256), ("Kbar", Kbar, Kbar[:], 256),
                       ("Bbar", Bbar, Bbar[:], 256), ("FMo", FMo, FMo[:].rearrange("p a b c -> p (a b c)"), 1024), ("NQ", NQ, NQ[:].rearrange("p a b -> p (a b)"), 512),
                       ("GH", GH, GH[:].rearrange("p a b -> p (a b)"), 512), ("Zf", Zf, Zf[:].rearrange("p a b -> p (a b)"), 768),
                       ("RpT", RpT, RpT[:].rearrange("p a b -> p (a b)"), 256), ("Phi", Phi, Phi[:].rearrange("p a b -> p (a b)"), 256)]
                dbgf = scratch(f"DBGF{c}", [64, sum(x[3] for x in f32l)], F32, dbg=True)
                dbgb = scratch(f"DBGB{c}", [64, sum(x[3] for x in bfl)], BF16, dbg=True)
                o = 0
                for (nm, tl, ap_, w) in f32l:
                    P.dma(dbgf[:, o:o + w], ap_, reads=[tl], writes=[dbgf]); o += w
                o = 0
                for (nm, tl, ap_, w) in bfl:
                    P.dma(dbgb[:, o:o + w], ap_, reads=[tl], writes=[dbgb]); o += w
            if isx:
                Yd = YF if d == 0 else YB
                P.dma(Yd[tok, :], ycur[:], reads=[ycur], writes=[Yd])
            yield

    gens = [run_pass(0), run_pass(1)]
    alive = [True, True]
    while any(alive):
        for k_ in range(2):
            if alive[k_]:
                try:
                    next(gens[k_])
                except StopIteration:
                    alive[k_] = False
    P.barrier()
    for c in range(4, NCH):
        tok = slice(c * 64, (c + 1) * 64)
        rkv = rkvs[2]
        (zt, logw, at, kk, sq, kka, kd, tt_, cums, ex, s4, Ahb, Rhb, Btb, Ktb, Kbar, Bbar, Vb, WCc,
         FMo, NQ, GH, MZ0, MZ1, Np0, Np1, Xneg, RpT, Phi, Y0s, PsiT, ycur, yfl, cen, af, outt) = [TSETS[c % 2][n] for n in SET_NAMES]
        P.dma(rkv[:], TM[tok, 512:1280], reads=[TM], writes=[rkv])
        r_, kr_, v_ = rkv[:, 0:256], rkv[:, 256:512], rkv[:, 512:768]
        P.dma(yfl[:], YF[tok, :], reads=[YF], writes=[yfl])
        P.dma(ycur[:], YB[tok, :], reads=[YB], writes=[ycur])
        P.op("dve", lambda e: e.tensor_tensor(ycur[:], ycur[:], yfl[:], ALU.add), reads=[ycur, yfl], writes=[ycur])
        P.op("dve", lambda e: e.tensor_reduce(s4[:, 8:12], h4(ycur), AX.X, ALU.add), reads=[ycur], writes=[s4])
        P.op("dve", lambda e: e.tensor_scalar(s4[:, 8:12], s4[:, 8:12], 1.0 / 64, None, ALU.mult), reads=[s4], writes=[s4])
        P.op("dve", lambda e: e.tensor_tensor(h4(cen), h4(ycur), bc_ap(s4[:, 8:12], 64), ALU.subtract), reads=[ycur, s4], writes=[cen])
        P.op("dve", lambda e: e.tensor_tensor(sq[:], cen[:], cen[:], ALU.mult), reads=[cen], writes=[sq])
        P.op("dve", lambda e: e.tensor_reduce(s4[:, 12:16], h4(sq), AX.X, ALU.add), reads=[sq], writes=[s4])
        P.op("dve", lambda e: e.tensor_scalar(s4[:, 12:16], s4[:, 12:16], 1.0 / 64, 64e-5, ALU.mult, ALU.add), reads=[s4], writes=[s4])
        P.op("act", lambda e: e.activation(s4[:, 12:16], s4[:, 12:16], AF.Sqrt), reads=[s4], writes=[s4])
        P.op("dve", lambda e: e.reciprocal(s4[:, 8:12], s4[:, 12:16]), reads=[s4], writes=[s4])
        P.op("dve", lambda e: e.tensor_tensor(h4(cen), h4(cen), bc_ap(s4[:, 8:12], 64), ALU.mult), reads=[cen, s4], writes=[cen])
        P.op("dve", lambda e: e.tensor_tensor(cen[:], cen[:], gnw, ALU.mult), reads=[cen, repr_], writes=[cen])
        P.op("dve", lambda e: e.tensor_tensor(cen[:], cen[:], gnb, ALU.add), reads=[cen, repr_], writes=[cen])
        b5, b6 = banks.next(), banks.next()
        P.mm(b5, b5[0:64, 0:256], laTs[0], laTs[0].full[:, tok], lads[0], lads[0].full[:])
        P.mm(b5, b5[0:64, 256:512], laTs[1], laTs[1].full[:, tok], lads[1], lads[1].full[:])
        P.mm(b6, b6[0:64, 0:256], sgb0, sgb0.full[:, tok], gub0, gub0.full[:], start=True, stop=False)
        P.mm(b6, b6[0:64, 0:256], sgb1, sgb1.full[:, tok], gub1, gub1.full[:], start=False, stop=True)
        P.op("dve", lambda e: e.tensor_tensor(af[:], b5[0:64, 0:256], a0r[0], ALU.add), reads=[b5, repr_], writes=[af])
        P.op("act", lambda e: e.activation(af[:], af[:], AF.Sigmoid), reads=[af], writes=[af])
        P.op("dve", lambda e: e.tensor_tensor(at[:], b5[0:64, 256:512], a0r[1], ALU.add), reads=[b5, repr_], writes=[at])
        P.op("act", lambda e: e.activation(at[:], at[:], AF.Sigmoid), reads=[at], writes=[at])
        P.op("dve", lambda e: e.tensor_tensor(af[:], af[:], at[:], ALU.add), reads=[af, at], writes=[af])
        P.op("dve", lambda e: e.tensor_tensor(af[:], af[:], rka, ALU.mult), reads=[af, repr_], writes=[af])
        P.op("dve", lambda e: e.scalar_tensor_tensor(af[:], omk[:], 2.0, af[:], ALU.mult, ALU.add), reads=[af, omk], writes=[af])
        P.op("dve", lambda e: e.tensor_tensor(af[:], af[:], kr_, ALU.mult), reads=[af, rkv], writes=[af])
        P.op("dve", lambda e: e.tensor_tensor(af[:], af[:], r_, ALU.mult), reads=[af, rkv], writes=[af])
        P.op("dve", lambda e: e.tensor_tensor(af[:], af[:], rrk, ALU.mult), reads=[af, repr_], writes=[af])
        P.op("dve", lambda e: e.tensor_reduce(s4[:, 12:16], h4(af), AX.X, ALU.add), reads=[af], writes=[s4])
        P.op("dve", lambda e: e.tensor_tensor(h4(af), v_.rearrange("p (h n) -> p h n", n=64), bc_ap(s4[:, 12:16], 64), ALU.mult), reads=[rkv, s4], writes=[af])
        P.op("dve", lambda e: e.tensor_tensor(cen[:], cen[:], af[:], ALU.add), reads=[cen, af], writes=[cen])
        P.op("dve", lambda e: e.tensor_tensor(outt[:], cen[:], b6[0:64, 0:256], ALU.mult), reads=[cen, b6], writes=[outt])
        P.dma(MIX[tok, 256:512], outt[:], reads=[outt], writes=[MIX])
    P.barrier()
    es3.close()
    if stage <= 3:
        es0.close()
        P.final_wait("sp")
        P.finish()
        return nc
    hold = {}
    P.raw("pool", lambda e: hold.__setitem__("pid", e.partition_id()))
    adaln([2, 3, 4, 5])
    P.op("dve", lambda e: e.scalar_tensor_tensor(drv[:, 4, :], modT[:, 4, :, 0], 1.0, pp[:, 16:32], ALU.add, ALU.mult), reads=[modT, pp], writes=[drv])
    P.op("dve", lambda e: e.tensor_copy(drv[:, 5, :], modT[:, 3, :, 0]), reads=[modT], writes=[drv])
    P.barrier()
    es0.close()
    es4a = ExitStack()
    es4 = es4a
    cvt = [P.sbuf(f"cvt{i}", [128, 512], es=es4) for i in range(2)]
    cvb = [P.sbuf(f"cvb{i}", [128, 512], BF16, es=es4) for i in range(4)]
    nb = 0
    for tt in range(32):
        a = cvt[tt % 2]
        P.dma(a[:], MIX[CTX + tt * 128:CTX + (tt + 1) * 128, :], reads=[MIX], writes=[a])
        for sl in range(8):
            bb_ = cvb[nb % 4]
            nb += 1
            P.op("act", lambda e, a=a, bb_=bb_, sl=sl: e.activation(bb_[:], a[:], AF.Identity, scale=pp[:, 48 + sl:49 + sl]), reads=[a, pp], writes=[bb_])
            r0_ = 64 + (sl // 4) * LX + tt * 128
            jj = sl % 4
            P.dma(CIN[r0_:r0_ + 128, jj * 256:(jj + 1) * 256], bb_[:, 0:256], reads=[bb_], writes=[CIN])
            P.dma(CIN[r0_:r0_ + 128, 1024 + jj * 256:1024 + (jj + 1) * 256], bb_[:, 256:512], reads=[bb_], writes=[CIN])
    P.barrier()
    P.coll(lambda e: e.collective_compute("AllReduce", ALU.add, replica_groups=[list(range(8))], ins=[cin_h.ap().opt()], outs=[COUT_h.ap().opt()]),
           reads=[CIN], writes=[COUT])
    P.barrier()
    es4a.close()
    es4 = ExitStack()
    gtrow = [P.sbuf(f"gtrow{i}", [128, D], es=es4) for i in range(2)]
    dg = P.sbuf("dg", [128, 4, 128], es=es4)
    for gi, m in enumerate((2, 5)):
        for c4 in range(4):
            bk = psr.next()
            for q in range(4):
                cc = c4 * 4 + q
                P.op("dve", lambda e, q=q, cc=cc, m=m: e.tensor_scalar(dg[:, q, :], cst[:, 0:128], modT[:, m, cc, 0:1], None, ALU.mult), reads=[cst, modT], writes=[dg])
                P.mm(bk, bk[:, q * 128:(q + 1) * 128], cst, cst[:, 384:512], dg, dg[:, q, :])
            P.op("act", lambda e, bk=bk, gi=gi, c4=c4: e.copy(gtrow[gi][:, c4 * 512:(c4 + 1) * 512], bk[:]), reads=[bk], writes=[gtrow[gi]])
    hx2T = P.sbuf("hx2T", [128, 16, 1152], BF16, es=es4)
    X1S = scratch("X1S", [1024, D], dbg=True)
    X1D = scratch("X1D", [1024, D], dbg=True)
    HX2D = scratch("HX2D", [128, 16 * 1152], BF16, dbg=True)
    ACTD = scratch("ACTD", [128, NF * 1024], BF16, dbg=True)
    YTD = scratch("YTD", [1152, D], BF16, dbg=True)
    es4c = ExitStack()
    wob = P.sbuf("wob", [128, 16, D], BF16, es=es4c)
    wst = [P.sbuf(f"wst{i}", [128, D], es=es4c) for i in range(2)]
    for kt in range(16):
        w_ = wst[kt % 2]
        P.dma(w_[:], w_out[kt * 128:(kt + 1) * 128, :], reads=[w_out], writes=[w_])
        P.op("pool", lambda e, kt=kt, w_=w_: e.tensor_copy(wob[:, kt, :], w_[:]), reads=[w_], writes=[wob])
    yts = [P.sbuf(f"yt{i}", [128, D], BF16, es=es4c) for i in range(2)]
    for y_ in yts:
        P.op("pool", lambda e, y_=y_: e.memset(y_[:], 0.0), writes=[y_])
    yTt = P.sbuf("yTt", [128, 16, 128], BF16, es=es4c)
    xt4 = [P.sbuf(f"xt4{i}", [128, D], es=es4c) for i in range(2)]
    for x_ in xt4:
        P.op("pool", lambda e, x_=x_: e.memset(x_[:], 0.0), writes=[x_])
    x1t = P.sbuf("x1t", [128, D], es=es4c); tmp4 = P.sbuf("tmp4", [128, 512], es=es4c)
    junk4 = P.sbuf("junk4", [128, D], BF16, es=es4c); xn4 = P.sbuf("xn4", [128, D], BF16, es=es4c); st4 = P.sbuf("st4", [128, 8], es=es4c)
    tiles = [(0, 64)] + [(64 + 128 * i, 128) for i in range(8)] + [(1088, 64)]
    for ti, (r0, nr) in enumerate(tiles):
        halo = nr == 64
        y_ = yts[0] if halo else yts[1]
        x_ = xt4[0] if halo else xt4[1]
        P.dma(y_[0:nr, :], lambda r0=r0, nr=nr: COUT_h[bass.ds(hold["pid"] * 1024 + r0, nr), :], reads=[COUT], writes=[y_], q="pool")
        P.dma(x_[0:nr, :], xtok[r0:r0 + nr, :], reads=[xtok], writes=[x_])
        if debug:
            P.dma(YTD[r0:r0 + nr, :], y_[0:nr, :], reads=[y_], writes=[YTD])
        for half in range(2):
            pbt = pb[half]
            for k8 in range(8):
                kt = half * 8 + k8
                P.tr(pbt, pbt[:, k8 * 128:(k8 + 1) * 128], y_, y_[:, kt * 128:(kt + 1) * 128], cstb, identb)
            P.op("act", lambda e, pbt=pbt, half=half: e.copy(yTt[:, half * 8:(half + 1) * 8, :].rearrange("p a b -> p (a b)"), pbt[:]), reads=[pbt], writes=[yTt])
        for cb in range(4):
            bk = psr.next()
            for kt in range(16):
                P.mm(bk, bk[:], yTt, yTt[:, kt, :], wob, wob[:, kt, cb * 512:(cb + 1) * 512], start=(kt == 0), stop=(kt == 15))
            P.op("dve", lambda e, bk=bk, cb=cb: e.tensor_tensor(tmp4[:], bk[:], gtrow[0][:, cb * 512:(cb + 1) * 512], ALU.mult), reads=[bk, gtrow[0]], writes=[tmp4])
            P.op("dve", lambda e, cb=cb, x_=x_: e.tensor_tensor(x1t[:, cb * 512:(cb + 1) * 512], tmp4[:], x_[:, cb * 512:(cb + 1) * 512], ALU.add), reads=[tmp4, x_], writes=[x1t])
        if not halo:
            P.dma(X1S[r0 - 64:r0 + 64, :], x1t[:], reads=[x1t], writes=[X1S])
            if debug:
                P.dma(X1D[r0 - 64:r0 + 64, :], x1t[:], reads=[x1t], writes=[X1D])
        norm_tile(x1t, 4, hx2T, r0, npart=128, st_=st4, xn_=xn4, junk_=junk4, ncols=nr)
    if debug:
        P.dma(HX2D[:], hx2T[:].rearrange("p a b -> p (a b)"), reads=[hx2T], writes=[HX2D])
    P.barrier()
    es4c.close()
    es4d = ExitStack()
    actT = P.sbuf("actT", [128, NF, 1024], BF16, es=es4d)
    es4e = ExitStack()
    fcw = P.sbuf("fcw_s", [128, NF, 10], es=es4e)
    P.dma(fcw[:], fcw_d[:], reads=[fcw_d], writes=[fcw])
    wus = [P.sbuf(f"wu{i}", [128, 16, 128], es=es4e) for i in range(1)] * 2; wgs = [P.sbuf(f"wg{i}", [128, 16, 128], es=es4e) for i in range(1)] * 2
    wub = [P.sbuf(f"wub{i}", [128, 16, 128], BF16, es=es4e) for i in range(2)]; wgb = [P.sbuf(f"wgb{i}", [128, 16, 128], BF16, es=es4e) for i in range(2)]
    u = P.sbuf("u", [128, 1152], es=es4e); acc4 = P.sbuf("acc4", [128, 1024], es=es4e); g1 = P.sbuf("g1", [128, 1024], es=es4e)
    u3 = u[:].rearrange("p (r c) -> p r c", c=64); a3 = acc4[:].rearrange("p (r c) -> p r c", c=64)
    for f in range(NF):
        wu_, wg_, wub_, wgb_ = wus[f % 2], wgs[f % 2], wub[f % 2], wgb[f % 2]
        P.dma(wu_[:], f_w_up[:, f * 128:(f + 1) * 128].rearrange("(kt p) c -> p kt c", p=128), reads=[f_w_up], writes=[wu_])
        P.dma(wg_[:], f_w_gate[:, f * 128:(f + 1) * 128].rearrange("(kt p) c -> p kt c", p=128), reads=[f_w_gate], writes=[wg_])
        P.op("pool", lambda e, wu_=wu_, wub_=wub_: e.tensor_copy(wub_[:], wu_[:]), reads=[wu_], writes=[wub_])
        P.op("pool", lambda e, wg_=wg_, wgb_=wgb_: e.tensor_copy(wgb_[:], wg_[:]), reads=[wg_], writes=[wgb_])
        for i in range(3):
            bk = psr.next()
            for kt in range(16):
                P.mm(bk, bk[:, 0:384], wub_, wub_[:, kt, :], hx2T, hx2T[:, kt, i * 384:(i + 1) * 384], start=(kt == 0), stop=(kt == 15))
            P.op("act", lambda e, bk=bk, i=i: e.copy(u[:, i * 384:(i + 1) * 384], bk[:, 0:384]), reads=[bk], writes=[u])
        gb = []
        for i in range(2):
            bk = (ps[4], psM)[i]
            for kt in range(16):
                P.mm(bk, bk[:], wgb_, wgb_[:, kt, :], hx2T, hx2T[:, kt, 64 + i * 512:64 + (i + 1) * 512], start=(kt == 0), stop=(kt == 15))
            gb.append(bk)
        P.op("dve", lambda e: e.tensor_scalar(u[:, 0:64], u[:, 0:64], pp[:, 41:42], None, ALU.mult), reads=[u, pp], writes=[u])
        P.op("dve", lambda e: e.tensor_scalar(u[:, 1088:1152], u[:, 1088:1152], pp[:, 42:43], None, ALU.mult), reads=[u, pp], writes=[u])
        P.op("dve", lambda e, f=f: e.tensor_scalar(acc4[:], u[:, 64:1088], fcw[:, f, 4:5], fcw[:, f, 9:10], ALU.mult, ALU.add), reads=[u, fcw], writes=[acc4])
        for ky in range(3):
            for kx in range(3):
                if ky == 1 and kx == 1:
                    continue
                dy, dx = ky - 1, kx - 1
                oc = slice(1, 64) if dx == -1 else (slice(0, 63) if dx == 1 else slice(0, 64))
                ic = slice(0, 63) if dx == -1 else (slice(1, 64) if dx == 1 else slice(0, 64))
                P.op("dve", lambda e, f=f, ky=ky, kx=kx, dy=dy, oc=oc, ic=ic: e.scalar_tensor_tensor(
                    a3[:, :, oc], u3[:, 1 + dy:17 + dy, ic], fcw[:, f, ky * 3 + kx:ky * 3 + kx + 1], a3[:, :, oc], ALU.mult, ALU.add),
                    reads=[u, fcw, acc4], writes=[acc4])
        P.op("act", lambda e: e.activation(g1[:], acc4[:], AF.Gelu_apprx_tanh), reads=[acc4], writes=[g1])
        for i in range(2):
            P.op("dve", lambda e, f=f, i=i, bk=gb[i]: e.tensor_tensor(actT[:, f, i * 512:(i + 1) * 512], g1[:, i * 512:(i + 1) * 512], bk[:], ALU.mult),
                 reads=[g1, gb[i]], writes=[actT])
    if debug:
        P.dma(ACTD[:], actT[:].rearrange("p a b -> p (a b)"), reads=[actT], writes=[ACTD])
    P.barrier()
    es4e.close()
    es4f = ExitStack()
    wds = [P.sbuf(f"wd{i}", [128, 512], es=es4f) for i in range(3)]; wdb = [P.sbuf(f"wdb{i}", [128, 512], BF16, es=es4f) for i in range(3)]
    x1r = [P.sbuf(f"x1r{i}", [128, 512], es=es4f) for i in range(2)]; tm5 = P.sbuf("tm5", [128, 512], es=es4f)
    fbanks = ps[0:5] + [psM]
    wi = 0
    for cb in range(4):
        for (tts) in (list(range(0, 6)), [6, 7]):
            for f in range(NF):
                wd_, wdb_ = wds[wi % 3], wdb[wi % 3]
                wi += 1
                P.dma(wd_[:], f_w_down[f * 128:(f + 1) * 128, cb * 512:(cb + 1) * 512], reads=[f_w_down], writes=[wd_])
                P.op("pool" if f % 2 == 0 else "act", (lambda e, wd_=wd_, wdb_=wdb_: e.tensor_copy(wdb_[:], wd_[:])) if f % 2 == 0 else
                     (lambda e, wd_=wd_, wdb_=wdb_: e.copy(wdb_[:], wd_[:])), reads=[wd_], writes=[wdb_])
                for k, tt in enumerate(tts):
                    bk = fbanks[k]
                    P.mm(bk, bk[:], actT, actT[:, f, tt * 128:(tt + 1) * 128], wdb_, wdb_[:], start=(f == 0), stop=(f == NF - 1))
            for k, tt in enumerate(tts):
                bk = fbanks[k]
                xr = x1r[k % 2]
                P.dma(xr[:], X1S[tt * 128:(tt + 1) * 128, cb * 512:(cb + 1) * 512], reads=[X1S], writes=[xr])
                P.op("dve", lambda e, bk=bk, cb=cb: e.tensor_tensor(tm5[:], bk[:], gtrow[1][:, cb * 512:(cb + 1) * 512], ALU.mult), reads=[bk, gtrow[1]], writes=[tm5])
                P.op("dve", lambda e, xr=xr: e.tensor_tensor(xr[:], tm5[:], xr[:], ALU.add), reads=[tm5, xr], writes=[xr])
                P.dma(X1S[tt * 128:(tt + 1) * 128, cb * 512:(cb + 1) * 512], xr[:], reads=[xr], writes=[X1S])
    P.barrier()
    es4f.close()
    es4f = ExitStack()
    gfin = P.sbuf("gfin_s", [128, D], es=es4f)
    P.dma(gfin[:], gfin_d[:], reads=[gfin_d], writes=[gfin])
    x2t = [P.sbuf(f"x2t{i}", [128, D], es=es4f) for i in range(1)] * 2; ot = [P.sbuf(f"ot{i}", [128, D], es=es4f) for i in range(1)] * 2
    jk6 = P.sbuf("jk6", [128, D], BF16, es=es4f); st6 = P.sbuf("st6", [128, 8], es=es4f)
    for tt in range(8):
        x_, o_ = x2t[tt % 2], ot[tt % 2]
        P.dma(x_[:], X1S[tt * 128:(tt + 1) * 128, :], reads=[X1S], writes=[x_])
        P.op("act", lambda e, x_=x_: e.activation(jk6[:], x_[:], AF.Square, accum_out=st6[:, 0:1]), reads=[x_], writes=[jk6, st6])
        P.op("dve", lambda e: e.tensor_scalar(st6[:, 1:2], st6[:, 0:1], 1.0 / D, EPS, ALU.mult, ALU.add), reads=[st6], writes=[st6])
        P.op("act", lambda e: e.activation(st6[:, 2:3], st6[:, 1:2], AF.Sqrt), reads=[st6], writes=[st6])
        P.op("dve", lambda e: e.reciprocal(st6[:, 3:4], st6[:, 2:3]), reads=[st6], writes=[st6])
        P.op("dve", lambda e, x_=x_, o_=o_: e.scalar_tensor_tensor(o_[:], x_[:], st6[:, 3:4], gfin[:], ALU.mult, ALU.mult), reads=[x_, st6, gfin], writes=[o_])
        P.dma(out_d[tt * 128:(tt + 1) * 128, :], o_[:], reads=[o_], writes=[out_d])
    P.barrier()
    es4f.close()
    es4d.close()
    es4.close()
    P.final_wait("sp")
    P.finish()
    return nc


def kernel(**inputs):
    inp = {k: np.asarray(v) for k, v in inputs.items()}
    nc = build(stage=4, debug=False)
    maps = [host_prep(inp, c) for c in range(8)]
    res = run_bass_kernel_spmd(nc, maps, core_ids=list(range(8)))
    out = np.zeros((2, LX, D), np.float32)
    for c in range(8):
        b, j = c // 4, c % 4
        out[b, 1024 * j:1024 * (j + 1)] = np.asarray(res.results[c]["out"], dtype=np.float32)
    return out
```
